# Optimizing a Trainium2 kernel written in Bass

```python
import jax, jax.numpy as jnp
from jax import lax
import numpy as np

D_MODEL = 2048
BATCH = 8
SEQ = 2048
DEPTH = 1
DEC_BATCH = 32
DEC_SEQ = 4
PAST_LEN = 8192
PAGE_SIZE = 128

MIX_DIM = D_MODEL
CONV_DIM = MIX_DIM // 2
ATTN_DIM = MIX_DIM - CONV_DIM
HEAD_DIM = 128
N_HEADS = ATTN_DIM // HEAD_DIM
CONV_W = 3
D_FF = 256 * ((8 * D_MODEL // 3 + 255) // 256)
PLE_DIM = 256
Q_BLOCK = 128
N_IN = 3 * CONV_DIM + 3 * ATTN_DIM + N_HEADS
EPS = 1e-6
SCALE = HEAD_DIM ** -0.5

kernel_name = "hymba_fox_shortconv_macaron_step"


def rms_norm(x, g):
    xf = x.astype(jnp.float32)
    y = xf * lax.rsqrt(jnp.mean(xf * xf, axis=-1, keepdims=True) + EPS)
    return (y * g.astype(jnp.float32)).astype(x.dtype)


def ffn_half(h, g, wg, wu, wd):
    hn = rms_norm(h, g)
    return h + 0.5 * ((jax.nn.silu(hn @ wg) * (hn @ wu)) @ wd)


def ple_add(h, p, g, wg, wp):
    hn = rms_norm(h, g)
    return h + jax.nn.sigmoid(hn @ wg) * (p.astype(h.dtype) @ wp)


def mixer_in(h, g, w_in, b_f):
    hn = rms_norm(h, g)
    z = hn @ w_in
    idx = [CONV_DIM, 2 * CONV_DIM, 3 * CONV_DIM,
           3 * CONV_DIM + ATTN_DIM, 3 * CONV_DIM + 2 * ATTN_DIM, 3 * CONV_DIM + 3 * ATTN_DIM]
    cb, cc, ch, q, k, v, fl = jnp.split(z, idx, axis=-1)
    lead = h.shape[:-1]
    q = q.reshape(*lead, N_HEADS, HEAD_DIM)
    k = k.reshape(*lead, N_HEADS, HEAD_DIM)
    v = v.reshape(*lead, N_HEADS, HEAD_DIM)
    logf = jax.nn.log_sigmoid(fl.astype(jnp.float32) + b_f.astype(jnp.float32))
    return cb, cc * ch, q, k, v, logf


def short_conv(u_ext, w):
    T = u_ext.shape[1] - (CONV_W - 1)
    y = w[0] * u_ext[:, 0:T]
    for j in range(1, CONV_W):
        y = y + w[j] * u_ext[:, j:j + T]
    return y


def mixer_out(h, conv_y, attn_o, gc, ga, w_out):
    lead = attn_o.shape[:-2]
    z = jnp.concatenate([rms_norm(conv_y, gc),
                         rms_norm(attn_o.reshape(*lead, ATTN_DIM), ga)], axis=-1)
    return h + z @ w_out


def fox_prompt(q, k, v, logf):
    B_, S_ = q.shape[0], q.shape[1]
    nb = S_ // Q_BLOCK
    F = jnp.cumsum(logf, axis=1)
    Fk = F.transpose(0, 2, 1)
    kpos = jnp.arange(S_)
    qb = q.reshape(B_, nb, Q_BLOCK, N_HEADS, HEAD_DIM).transpose(1, 0, 2, 3, 4)
    Fq = F.reshape(B_, nb, Q_BLOCK, N_HEADS).transpose(1, 0, 3, 2)
    qpos = jnp.arange(S_).reshape(nb, Q_BLOCK)

    def block(args):
        q_i, F_i, pos_i = args
        s = jnp.einsum('bqhd,bkhd->bhqk', q_i, k, preferred_element_type=jnp.float32) * SCALE
        s = s + F_i[..., None] - Fk[:, :, None, :]
        s = jnp.where(kpos[None, None, None, :] <= pos_i[None, None, :, None], s, -jnp.inf)
        p = jax.nn.softmax(s, axis=-1)
        return jnp.einsum('bhqk,bkhd->bqhd', p.astype(v.dtype), v)

    o = lax.map(block, (qb, Fq, qpos))
    return o.transpose(1, 0, 2, 3, 4).reshape(B_, S_, N_HEADS, HEAD_DIM)


def fox_sample(q, k_new, v_new, logf_new, cache_k, cache_v, cache_logf, page_table):
    DB, T = q.shape[0], q.shape[1]
    P = page_table.shape[1] * PAGE_SIZE
    k_past = cache_k[page_table].reshape(DB, P, N_HEADS, HEAD_DIM)
    v_past = cache_v[page_table].reshape(DB, P, N_HEADS, HEAD_DIM)
    lf_past = cache_logf[page_table].reshape(DB, P, N_HEADS)
    k_all = jnp.concatenate([k_past.astype(k_new.dtype), k_new], axis=1)
    v_all = jnp.concatenate([v_past.astype(v_new.dtype), v_new], axis=1)
    lf_all = jnp.concatenate([lf_past.astype(jnp.float32), logf_new], axis=1)
    F = jnp.cumsum(lf_all, axis=1).transpose(0, 2, 1)
    Fq = F[:, :, P:]
    s = jnp.einsum('bqhd,bkhd->bhqk', q, k_all, preferred_element_type=jnp.float32) * SCALE
    s = s + Fq[..., None] - F[:, :, None, :]
    qpos = P + jnp.arange(T)
    kpos = jnp.arange(P + T)
    s = jnp.where(kpos[None, None, None, :] <= qpos[None, None, :, None], s, -jnp.inf)
    p = jax.nn.softmax(s, axis=-1)
    return jnp.einsum('bhqk,bkhd->bqhd', p.astype(v_all.dtype), v_all)


def setup_inputs(seed: int = 0) -> dict:
    key = jax.random.key(seed)
    ks = jax.random.split(key, 32)
    n_pages = PAST_LEN // PAGE_SIZE
    n_used = DEC_BATCH * n_pages
    n_pool = n_used + max(1, n_used // 4)
    nrm = jax.random.normal
    f32 = jnp.float32
    page_table = jax.random.permutation(ks[0], n_pool)[:n_used].reshape(DEC_BATCH, n_pages).astype(jnp.int32)

    def gain(k, n):
        return 1.0 + 0.05 * nrm(k, (DEPTH, n), f32)

    return {
        "x_prompt": nrm(ks[1], (BATCH, SEQ, D_MODEL), f32),
        "x_sample": nrm(ks[2], (DEC_BATCH, DEC_SEQ, D_MODEL), f32),
        "p_prompt": nrm(ks[3], (DEPTH, BATCH, SEQ, PLE_DIM), f32),
        "p_sample": nrm(ks[4], (DEPTH, DEC_BATCH, DEC_SEQ, PLE_DIM), f32),
        "cache_k": nrm(ks[5], (DEPTH, n_pool, PAGE_SIZE, N_HEADS, HEAD_DIM), f32),
        "cache_v": nrm(ks[6], (DEPTH, n_pool, PAGE_SIZE, N_HEADS, HEAD_DIM), f32),
        "cache_logf": jax.nn.log_sigmoid(2.0 + nrm(ks[7], (DEPTH, n_pool, PAGE_SIZE, N_HEADS), f32)),
        "state_conv": nrm(ks[8], (DEPTH, DEC_BATCH, CONV_W - 1, CONV_DIM), f32),
        "page_table": page_table,
        "norm_ffn1": gain(ks[9], D_MODEL),
        "w_ffn1_gate": nrm(ks[10], (DEPTH, D_MODEL, D_FF), f32) * D_MODEL ** -0.5,
        "w_ffn1_up": nrm(ks[11], (DEPTH, D_MODEL, D_FF), f32) * D_MODEL ** -0.5,
        "w_ffn1_down": nrm(ks[12], (DEPTH, D_FF, D_MODEL), f32) * D_FF ** -0.5,
        "norm_mix": gain(ks[13], D_MODEL),
        "w_in": nrm(ks[14], (DEPTH, D_MODEL, N_IN), f32) * D_MODEL ** -0.5,
        "b_f": 2.0 + 0.5 * nrm(ks[15], (DEPTH, N_HEADS), f32),
        "conv_w": nrm(ks[16], (DEPTH, CONV_W, CONV_DIM), f32) * CONV_W ** -0.5,
        "norm_conv_out": gain(ks[17], CONV_DIM),
        "norm_attn_out": gain(ks[18], ATTN_DIM),
        "w_out": nrm(ks[19], (DEPTH, MIX_DIM, D_MODEL), f32) * MIX_DIM ** -0.5,
        "norm_ffn2": gain(ks[20], D_MODEL),
        "w_ffn2_gate": nrm(ks[21], (DEPTH, D_MODEL, D_FF), f32) * D_MODEL ** -0.5,
        "w_ffn2_up": nrm(ks[22], (DEPTH, D_MODEL, D_FF), f32) * D_MODEL ** -0.5,
        "w_ffn2_down": nrm(ks[23], (DEPTH, D_FF, D_MODEL), f32) * D_FF ** -0.5,
        "norm_ple": gain(ks[24], D_MODEL),
        "w_ple_gate": nrm(ks[25], (DEPTH, D_MODEL, D_MODEL), f32) * D_MODEL ** -0.5,
        "w_ple_proj": nrm(ks[26], (DEPTH, PLE_DIM, D_MODEL), f32) * PLE_DIM ** -0.5,
        "norm_final": 1.0 + 0.05 * nrm(ks[27], (D_MODEL,), f32),
    }


def reference(x_prompt, x_sample, p_prompt, p_sample, cache_k, cache_v, cache_logf, state_conv,
              page_table, norm_ffn1, w_ffn1_gate, w_ffn1_up, w_ffn1_down, norm_mix, w_in, b_f,
              conv_w, norm_conv_out, norm_attn_out, w_out, norm_ffn2, w_ffn2_gate, w_ffn2_up,
              w_ffn2_down, norm_ple, w_ple_gate, w_ple_proj, norm_final):
    hp, hs = x_prompt, x_sample
    kp_l, vp_l, lfp_l, cp_l = [], [], [], []
    ks_l, vs_l, lfs_l, cs_l = [], [], [], []
    for l in range(DEPTH):
        hp = ffn_half(hp, norm_ffn1[l], w_ffn1_gate[l], w_ffn1_up[l], w_ffn1_down[l])
        cb, u, q, k, v, logf = mixer_in(hp, norm_mix[l], w_in[l], b_f[l])
        u_ext = jnp.pad(u, ((0, 0), (CONV_W - 1, 0), (0, 0)))
        conv_y = cb * short_conv(u_ext, conv_w[l])
        attn_o = fox_prompt(q, k, v, logf)
        hp = mixer_out(hp, conv_y, attn_o, norm_conv_out[l], norm_attn_out[l], w_out[l])
        hp = ffn_half(hp, norm_ffn2[l], w_ffn2_gate[l], w_ffn2_up[l], w_ffn2_down[l])
        hp = ple_add(hp, p_prompt[l], norm_ple[l], w_ple_gate[l], w_ple_proj[l])
        kp_l.append(k)
        vp_l.append(v)
        lfp_l.append(logf)
        cp_l.append(u_ext[:, -(CONV_W - 1):])

        hs = ffn_half(hs, norm_ffn1[l], w_ffn1_gate[l], w_ffn1_up[l], w_ffn1_down[l])
        cb, u, q, k, v, logf = mixer_in(hs, norm_mix[l], w_in[l], b_f[l])
        u_ext = jnp.concatenate([state_conv[l].astype(u.dtype), u], axis=1)
        conv_y = cb * short_conv(u_ext, conv_w[l])
        attn_o = fox_sample(q, k, v, logf, cache_k[l], cache_v[l], cache_logf[l], page_table)
        hs = mixer_out(hs, conv_y, attn_o, norm_conv_out[l], norm_attn_out[l], w_out[l])
        hs = ffn_half(hs, norm_ffn2[l], w_ffn2_gate[l], w_ffn2_up[l], w_ffn2_down[l])
        hs = ple_add(hs, p_sample[l], norm_ple[l], w_ple_gate[l], w_ple_proj[l])
        ks_l.append(k)
        vs_l.append(v)
        lfs_l.append(logf)
        cs_l.append(u_ext[:, -(CONV_W - 1):])

    y_prompt = rms_norm(hp, norm_final)
    y_sample = rms_norm(hs, norm_final)
    return (y_prompt, y_sample,
            jnp.stack(kp_l), jnp.stack(vp_l), jnp.stack(lfp_l), jnp.stack(cp_l),
            jnp.stack(ks_l), jnp.stack(vs_l), jnp.stack(lfs_l), jnp.stack(cs_l))
```

```python
import contextlib
import numpy as np
import concourse.bass as bass
import concourse.mybir as mybir
from concourse.bass_utils import run_bass_kernel_spmd

F32 = mybir.dt.float32
BF16 = mybir.dt.bfloat16
AF = mybir.ActivationFunctionType
ALU = mybir.AluOpType

D = 2048
KC = 16
DFF = 5632
NFC = 44
G = 4
NG = NFC // G
CONV = 1024
NH = 8
HD = 128
PLE = 256
SEQ = 2048
NBLK = 4
NT = 512
WITH_SAMPLE = True
NS = 16
EPS = 1e-6
SCALE = HD ** -0.5
RING = 8
NSTG = 3
LOOK = 10


def _fm_units(W, col0):
    blk = W[:, col0:col0 + 128].reshape(16, 128, 128)
    return [np.ascontiguousarray(blk[half * 8:(half + 1) * 8].transpose(1, 0, 2)).reshape(128, 1024)
            for half in range(2)]


def pack_weights(inp, core):
    units = []

    def ffn(wg, wu, wd):
        for g in range(NG):
            for j in range(G):
                fc = g * G + j
                units.extend(_fm_units(wg, fc * 128))
                units.extend(_fm_units(wu, fc * 128))
            for half in range(2):
                for j in range(G):
                    fc = g * G + j
                    units.append(np.ascontiguousarray(wd[fc * 128:(fc + 1) * 128, half * 1024:(half + 1) * 1024]))

    ffn(inp["w_ffn1_gate"][0], inp["w_ffn1_up"][0], inp["w_ffn1_down"][0])
    win = inp["w_in"][0]
    for i in range(8):
        units.extend(_fm_units(win, 1024 + 128 * i))
        units.extend(_fm_units(win, 2048 + 128 * i))
        units.extend(_fm_units(win, 128 * i))
    for h in range(NH):
        units.extend(_fm_units(win, 4096 + 128 * h))
    for cb in range(2):
        col0 = 5120 + 512 * cb
        blk = win[:, col0:col0 + 512].reshape(16, 128, 512)
        for u in range(8):
            units.append(np.ascontiguousarray(blk[2 * u:2 * u + 2].transpose(1, 0, 2)).reshape(128, 1024))
    for h in range(NH):
        units.extend(_fm_units(win, 3072 + 128 * h))
    wo = inp["w_out"][0]
    for dc in range(16):
        units.extend(_fm_units(wo, dc * 128))
    ffn(inp["w_ffn2_gate"][0], inp["w_ffn2_up"][0], inp["w_ffn2_down"][0])
    wpg = inp["w_ple_gate"][0]
    wpp = inp["w_ple_proj"][0]
    for q in range(4):
        blk = wpp[:, q * 512:(q + 1) * 512].reshape(2, 128, 4, 128)
        units.append(np.ascontiguousarray(blk.transpose(1, 2, 0, 3)).reshape(128, 1024))
        for dcl in range(4):
            units.extend(_fm_units(wpg, (4 * q + dcl) * 128))
    assert len(units) == NU, len(units)
    return np.stack(units, axis=0)


NU_FFN = NG * (G * 4 + G * 2)
NU_IN = 8 * 6 + NH * 2 + 16
NU = 2 * NU_FFN + NU_IN + NH * 2 + 32 + 4 + 32
NX = 6


def pack_extras(inp, core):
    win = inp["w_in"][0]
    return np.stack(_fm_units(win, 3072 + 128 * core) + _fm_units(win, 4096 + 128 * core) +
                    _fm_units(win, 5120 + 128 * core), axis=0)


PC = {}
_o = 0
for _n, _w in [("g1", 16), ("gm", 16), ("g2", 16), ("gp", 16), ("gf", 16), ("gc", 8), ("ga", 8), ("cw", 24),
               ("bf", 8), ("hsel", 8), ("iota", 1), ("ident", 128), ("tri", 128), ("last", 128), ("mask", 128), ("ustr", 128),
               ("maskS", 128), ("btri", 128), ("sel", 1024)]:
    PC[_n] = (_o, _w)
    _o += _w
NPAR = _o


def pack_params(inp, core):
    P = np.zeros((128, NPAR), np.float32)

    def put(name, arr):
        o, w = PC[name]
        P[:, o:o + w] = arr

    put("g1", inp["norm_ffn1"][0].reshape(16, 128).T)
    put("gm", inp["norm_mix"][0].reshape(16, 128).T)
    put("g2", inp["norm_ffn2"][0].reshape(16, 128).T)
    put("gp", inp["norm_ple"][0].reshape(16, 128).T)
    put("gf", inp["norm_final"].reshape(16, 128).T)
    put("gc", inp["norm_conv_out"][0].reshape(8, 128).T)
    put("ga", inp["norm_attn_out"][0].reshape(8, 128).T)
    put("cw", inp["conv_w"][0].reshape(3, 8, 128).transpose(2, 0, 1).reshape(128, 24))
    put("bf", np.broadcast_to(inp["b_f"][0][None, :], (128, 8)))
    hs = np.zeros((128, 8), np.float32)
    hs[:, core] = 1.0
    put("hsel", hs)
    put("iota", np.arange(128, dtype=np.float32)[:, None])
    put("ident", np.eye(128, dtype=np.float32))
    ar = np.arange(128)
    put("tri", (ar[:, None] <= ar[None, :]).astype(np.float32))
    last = np.zeros((128, 128), np.float32)
    last[127, :] = 1.0
    put("last", last)
    put("mask", np.where(ar[:, None] <= ar[None, :], 0.0, -30000.0).astype(np.float32))
    put("ustr", (ar[:, None] > ar[None, :]).astype(np.float32))
    same = (ar[:, None] // 4) == (ar[None, :] // 4)
    put("maskS", np.where(same & (ar[:, None] <= ar[None, :]), 0.0, -30000.0).astype(np.float32))
    put("btri", (same & (ar[:, None] <= ar[None, :])).astype(np.float32))
    sel = np.zeros((128, 8, 128), np.float32)
    for h in range(8):
        sel[h, h, :] = 1.0
    put("sel", sel.reshape(128, 1024))
    return P


class Sched:
    def __init__(self):
        self.ops = {e: [] for e in ("pe", "act", "dve", "pool", "sp")}
        self.regions = {}
        self.dma_cnt = {}

    def add(self, eng, fn, reads=(), writes=(), dma=None):
        deps = []
        for r in reads:
            reg = self.regions.get(r)
            if reg is not None and reg["w"] is not None:
                deps.append(reg["w"])
        for w in writes:
            reg = self.regions.get(w)
            if reg is not None:
                if reg["w"] is not None:
                    deps.append(reg["w"])
                deps.extend(reg["r"].values())
        idx = len(self.ops[eng])
        if dma is not None:
            k = self.dma_cnt.get(dma, 0) + 1
            self.dma_cnt[dma] = k
            ref = ("dma", dma, k)
            key = "dma:" + dma
        else:
            ref = ("eng", eng, idx)
            key = eng
        if eng == "pe":
            deps = [d for d in deps if not (d[0] == "eng" and d[1] == "pe")]
        self.ops[eng].append({"fn": fn, "deps": deps, "marked": False, "dma": dma})
        for r in reads:
            reg = self.regions.setdefault(r, {"w": None, "r": {}})
            reg["r"][key] = ref
        for w in writes:
            self.regions[w] = {"w": ref, "r": {}}
        return ref

    def emit(self, nc, engines, esem, dsem, final_waits):
        for e, lst in self.ops.items():
            for op in lst:
                for d in op["deps"]:
                    if d[0] == "eng":
                        self.ops[d[1]][d[2]]["marked"] = True
        cum = {}
        for e, lst in self.ops.items():
            c = 0
            arr = []
            for op in lst:
                if op["marked"]:
                    c += 1
                arr.append(c)
            cum[e] = arr

        def run(e, eng):
            waited = {}
            for i, op in enumerate(self.ops[e]):
                need = {}
                for d in op["deps"]:
                    if d[0] == "eng":
                        if d[1] == e and d[2] >= i:
                            continue
                        s, v = ("e", d[1]), cum[d[1]][d[2]]
                    else:
                        s, v = ("d", d[1]), 16 * d[2]
                    if v > need.get(s, 0):
                        need[s] = v
                for s, v in need.items():
                    if waited.get(s, 0) >= v:
                        continue
                    waited[s] = v
                    sem = esem[s[1]] if s[0] == "e" else dsem[s[1]]
                    eng.wait_ge(sem, v)
                ins = op["fn"](eng)
                if op["dma"] is not None:
                    ins.then_inc(dsem[op["dma"]], 16)
                elif op["marked"]:
                    ins.then_inc(esem[e], 1)
            if e == "sp":
                for name in final_waits:
                    eng.wait_ge(dsem[name], 16 * self.dma_cnt[name])

        with nc.Block() as block:
            @block.sync
            def _(eng):
                run("sp", eng)

            @block.tensor
            def _(eng):
                run("pe", eng)

            @block.scalar
            def _(eng):
                run("act", eng)

            @block.vector
            def _(eng):
                run("dve", eng)

            @block.gpsimd
            def _(eng):
                run("pool", eng)


NPOOL = 2560
NPG = 64
PCH = 4


def build_program():
    nc = bass.Bass("TRN2", target_bir_lowering=False)
    S = Sched()
    dt = nc.dram_tensor
    I32 = mybir.dt.int32
    xT = dt("xT", [D, SEQ], F32, kind="ExternalInput").ap()
    pT = dt("pT", [PLE, SEQ], F32, kind="ExternalInput").ap()
    xsT = dt("xsT", [D, NS], F32, kind="ExternalInput").ap()
    psT = dt("psT", [PLE, NS], F32, kind="ExternalInput").ap()
    stc = dt("stc", [128, 64], F32, kind="ExternalInput").ap()
    wst = dt("wst", [NU, 128, 1024], F32, kind="ExternalInput").ap()
    par = dt("par", [128, NPAR], F32, kind="ExternalInput").ap()
    wfd = dt("wfd", [128, KC * 8], F32, kind="ExternalInput").ap()
    if WITH_SAMPLE:
        ckv = dt("ckv", [NH * NPOOL * 128, 256], F32, kind="ExternalInput").ap()
        clf = dt("clf", [NH * NPOOL, 128], F32, kind="ExternalInput").ap()
        ptab = dt("ptab", [4, NPG], I32, kind="ExternalInput").ap()
    yT = dt("yT", [D, SEQ], F32, kind="ExternalOutput").ap()
    kT = dt("kT", [CONV, SEQ], F32, kind="ExternalOutput").ap()
    vo = dt("vo", [SEQ, CONV], F32, kind="ExternalOutput").ap()
    lfo = dt("lfo", [SEQ, NH], F32, kind="ExternalOutput").ap()
    cvo = dt("cvo", [128, 16], F32, kind="ExternalOutput").ap()
    ysT = dt("ysT", [D, NS], F32, kind="ExternalOutput").ap()
    ksT = dt("ksT", [CONV, NS], F32, kind="ExternalOutput").ap()
    vso = dt("vso", [NS, CONV], F32, kind="ExternalOutput").ap()
    lfso = dt("lfso", [NS, NH], F32, kind="ExternalOutput").ap()
    cvso = dt("cvso", [128, 64], F32, kind="ExternalOutput").ap()

    es = contextlib.ExitStack()
    with es:
        def sb(name, shape, dtype):
            return es.enter_context(nc.sbuf_tensor(name, shape, dtype))

        prm = sb("prm", [128, NPAR], F32)
        onesb = sb("onesb", [128, 128], BF16)
        onesf = sb("onesf", [128, 128], F32)
        identb = sb("identb", [128, 128], BF16)
        maskb = sb("maskb", [128, 128], BF16)
        selb = sb("selb", [8, 1024], BF16)
        epst = sb("epst", [128, 1], F32)
        h = sb("h", [128, KC, NT], F32)
        hn = sb("hn", [128, KC, NT], BF16)
        sqr = sb("sqr", [128, 2, NT], BF16)
        rs = sb("rs", [128, NT], F32)
        rstd = sb("rstd", [128, NT], F32)
        sg = sb("sg", [128, 2, NT], BF16)
        abuf = sb("abuf", [128, G, NT], BF16)
        stg = sb("stg", [128, NSTG, 1024], F32)
        ring = sb("ring", [128, RING, 1024], BF16)
        ccs = sb("ccs", [128, NT], F32)
        cbs = sb("cbs", [128, NT], F32)
        y1 = sb("y1", [128, NT], F32)
        ubP = sb("ubP", [128, 1, NT + 2], F32)
        ubS = sb("ubS", [128, 4, 6], F32)
        uhist = sb("uhist", [128, 8, 2], F32)
        sth = sb("sth", [128, 64], F32)
        cvs = sb("cvs", [128, 64], F32)
        kst = sb("kst", [128, 2, NT], F32)
        vst = sb("vst", [128, 2, NT], F32)
        zb = sb("zb", [128, KC, NT], BF16)
        qT = sb("qT", [128, 2, NT], F32)
        pTt = sb("pTt", [128, 3, NT], F32)
        kvk = sb("kvk", [128, 4, 128], F32)
        kvv = sb("kvv", [128, 4, 128], F32)
        rden = sb("rden", [128, NT], F32)
        lft = sb("lft", [128, 2, 8], F32)
        lfe = sb("lfe", [128, 8], F32)
        lfS = sb("lfS", [128, 8], F32)
        Ftm = sb("Ftm", [128, KC + 1, 8], F32)
        negF = sb("negF", [128, KC, 8], F32)
        Fx = sb("Fx", [8, NT], F32)
        Fr = sb("Fr", [8, NT], F32)
        Fhi = sb("Fhi", [8, 3, NT], BF16)
        wfb = sb("wfb", [128, KC, 8], BF16)
        wfs = sb("wfs", [128, KC, 8], F32)
        ptb = sb("ptb", [128, 2, NT], BF16)
        ptbc = sb("ptbc", [128, 2, NPG], I32)
        idxb = sb("idxb", [128, 2, NPG], I32)
        pcolb = sb("pcolb", [64, 2], I32)
        Lb = sb("Lb", [64, 128], F32)
        LT = sb("LT", [128, 64], F32)
        Dm = sb("Dm", [128, 64], F32)
        Csc = sb("Csc", [128, 64], F32)
        ones64 = sb("ones64", [128, 64], F32)
        kvp = sb("kvp", [128, 2, PCH, 256], F32)
        Pm = sb("Pm", [128, 2, PCH * 4], F32)
        tmpS = sb("tmpS", [128, PCH * 4], F32)
        qTs = sb("qTs", [128, NH, NS], F32)
        knT = sb("knT", [128, NH, NS], F32)
        vnw = sb("vnw", [NS, CONV], F32)
        negE = sb("negE", [NS, 8], F32)
        osb = sb("osb", [128, 128], F32)
        idx0f = sb("idx0f", [128, NPG], F32)
        idxh = sb("idxh", [128, NH, NPG], I32)
        pcolh = sb("pcolh", [64, NH], I32)
        psum = es.enter_context(nc.psum_tensor("psum", [128, 8, 512], F32))

        esem = {e: es.enter_context(nc.semaphore("es_" + e)) for e in ("pe", "act", "dve", "pool", "sp")}
        dsem = {}

        def pcol(name, i=0, w=1):
            o, _ = PC[name]
            return prm[:, o + i:o + i + w]

        bank_ctr = [0]

        def nb():
            b = bank_ctr[0] % 4
            bank_ctr[0] += 1
            return b

        acc_ctr = [0]

        def accpair():
            k = acc_ctr[0] % 2
            acc_ctr[0] += 1
            return 4 + 2 * k, 5 + 2 * k

        def dma(fn, reads=(), writes=(), sem=None, eng="sp"):
            if sem not in dsem:
                dsem[sem] = es.enter_context(nc.semaphore("ds_" + sem))
            return S.add(eng, fn, reads=reads, writes=writes, dma=sem)

        p_full = list(range(NU))
        cut = NU_FFN + NU_IN
        plan = p_full * (NBLK + 1)
        if not WITH_SAMPLE:
            plan = p_full * NBLK
        wstate = {"next_use": 0, "next_fetch": 0}

        def fetch_upto(u_hi):
            while wstate["next_fetch"] < min(u_hi, len(plan)):
                u = wstate["next_fetch"]
                wstate["next_fetch"] += 1
                s4, s16, ui = u % NSTG, u % RING, plan[u]
                srcu = wst[ui]
                dma(lambda e, s4=s4, srcu=srcu: e.dma_start(out=stg[:, s4, :], in_=srcu),
                    writes=["stg%d" % s4], sem="wst%d" % s4)
                S.add("pool", lambda e, s4=s4, s16=s16: e.tensor_copy(out=ring[:, s16, :], in_=stg[:, s4, :]),
                      reads=["stg%d" % s4], writes=["wr%d" % s16])

        def take(n):
            u0 = wstate["next_use"]
            wstate["next_use"] += n
            assert n <= RING
            fetch_upto(max(u0 + n, u0 + RING))
            return [(u % RING) for u in range(u0, u0 + n)]

        dma(lambda e: e.dma_start(out=prm[:], in_=par[:]), writes=["prm"], sem="prm")
        S.add("dve", lambda e: e.tensor_copy(out=identb[:], in_=pcol("ident", 0, 128)), reads=["prm"], writes=["identb"])
        S.add("dve", lambda e: e.tensor_copy(out=maskb[:], in_=pcol("mask", 0, 128)), reads=["prm"], writes=["maskb"])
        S.add("dve", lambda e: e.tensor_copy(out=selb[:], in_=prm[0:8, PC["sel"][0]:PC["sel"][0] + 1024]),
              reads=["prm"], writes=["selb"])
        S.add("dve", lambda e: e.memset(onesb[:], 1.0), writes=["onesb"])
        S.add("dve", lambda e: e.memset(onesf[:], 1.0), writes=["onesf"])
        S.add("dve", lambda e: e.memset(ones64[:], 1.0), writes=["ones64"])
        S.add("dve", lambda e: e.memset(epst[:], EPS), writes=["epst"])
        S.add("dve", lambda e: e.memset(Ftm[:, 0, :], 0.0), writes=["Ftm0"])
        S.add("dve", lambda e: e.memset(uhist[:], 0.0), writes=["uhist"])
        dma(lambda e: e.dma_start(out=wfs[:].rearrange("p a b -> p (a b)"), in_=wfd[:]), writes=["wfs"], sem="wf")
        S.add("dve", lambda e: e.tensor_copy(out=wfb[:], in_=wfs[:]), reads=["wfs"], writes=["wfb"])
        dma(lambda e: e.dma_start(out=sth[:], in_=stc[:]), writes=["sth"], sem="sth")

        def mm(out_ap, lhsT, rhs, start, stop, reads, bank):
            S.add("pe", lambda e: e.matmul(out_ap, lhsT, rhs, start=start, stop=stop),
                  reads=reads, writes=["ps%d" % bank])

        def sumsq_rstd(N, nchunks, src, srcreg, dim):
            b = nb()
            for c in range(nchunks):
                sl = c % 2
                S.add("act", lambda e, c=c, sl=sl: e.activation(out=sqr[:, sl, :N], in_=src[:, c, :N], func=AF.Square),
                      reads=["%s%d" % (srcreg, c)], writes=["sqr%d" % sl])
                mm(psum[:, b, :N], onesb[:], sqr[:, sl, :N], c == 0, c == nchunks - 1, ["sqr%d" % sl, "onesb"], b)
            S.add("act", lambda e: e.activation(out=rs[:, :N], in_=psum[:, b, :N], func=AF.Sqrt, bias=epst[:],
                                                scale=1.0 / dim), reads=["ps%d" % b, "epst"], writes=["rs"])
            S.add("dve", lambda e: e.reciprocal(out=rstd[:, :N], in_=rs[:, :N]), reads=["rs"], writes=["rstd"])

        def norm(N, gname, nchunks, src, srcreg, dst, dstreg, dim, c0=0):
            sv = src[:, c0:c0 + nchunks, :]
            sumsq_rstd(N, nchunks, sv, srcreg + "_" if False else srcreg, dim) if c0 == 0 else None
            if c0 != 0:
                b = nb()
                for c in range(nchunks):
                    sl = c % 2
                    S.add("act", lambda e, c=c, sl=sl: e.activation(out=sqr[:, sl, :N], in_=src[:, c0 + c, :N],
                                                                     func=AF.Square),
                          reads=["%s%d" % (srcreg, c0 + c)], writes=["sqr%d" % sl])
                    mm(psum[:, b, :N], onesb[:], sqr[:, sl, :N], c == 0, c == nchunks - 1, ["sqr%d" % sl, "onesb"], b)
                S.add("act", lambda e: e.activation(out=rs[:, :N], in_=psum[:, b, :N], func=AF.Sqrt, bias=epst[:],
                                                    scale=1.0 / dim), reads=["ps%d" % b, "epst"], writes=["rs"])
                S.add("dve", lambda e: e.reciprocal(out=rstd[:, :N], in_=rs[:, :N]), reads=["rs"], writes=["rstd"])
            for c in range(nchunks):
                S.add("dve", lambda e, c=c: e.scalar_tensor_tensor(out=dst[:, c0 + c, :N], in0=src[:, c0 + c, :N],
                                                                    scalar=pcol(gname, c), in1=rstd[:, :N],
                                                                    op0=ALU.mult, op1=ALU.mult),
                      reads=["%s%d" % (srcreg, c0 + c), "rstd", "prm"], writes=["%s%d" % (dstreg, c0 + c)])

        def fm_proj(N, units, src_tile, srcreg, nk=16):
            b = nb()
            for kc in range(nk):
                slot = units[kc // 8]
                mm(psum[:, b, :N], ring[:, slot, (kc % 8) * 128:(kc % 8 + 1) * 128], src_tile[:, kc, :N],
                   kc == 0, kc == nk - 1, ["wr%d" % slot, "%s%d" % (srcreg, kc)], b)
            return b

        def ffn(N, gname):
            norm(N, gname, KC, h, "h", hn, "hn", D)
            for g in range(NG):
                for j in range(G):
                    us = take(4)
                    bg = fm_proj(N, us[0:2], hn, "hn")
                    bu = fm_proj(N, us[2:4], hn, "hn")
                    par2 = j % 2
                    S.add("act", lambda e, bg=bg, par2=par2: e.activation(out=sg[:, par2, :N], in_=psum[:, bg, :N],
                                                                            func=AF.Silu),
                          reads=["ps%d" % bg], writes=["sg%d" % par2])
                    S.add("dve", lambda e, bu=bu, par2=par2, j=j: e.tensor_tensor(
                        out=abuf[:, j, :N], in0=sg[:, par2, :N], in1=psum[:, bu, :N], op=ALU.mult),
                        reads=["sg%d" % par2, "ps%d" % bu], writes=["a%d" % j])
                for half in range(2):
                    us = take(G)
                    for dcl in range(8):
                        dc = half * 8 + dcl
                        b = nb()
                        for j in range(G):
                            mm(psum[:, b, :N], ring[:, us[j], dcl * 128:(dcl + 1) * 128], abuf[:, j, :N],
                               j == 0, j == G - 1, ["wr%d" % us[j], "a%d" % j], b)
                        S.add("dve", lambda e, b=b, dc=dc: e.scalar_tensor_tensor(
                            out=h[:, dc, :N], in0=psum[:, b, :N], scalar=0.5, in1=h[:, dc, :N],
                            op0=ALU.mult, op1=ALU.add),
                            reads=["ps%d" % b, "h%d" % dc], writes=["h%d" % dc])

        def load_block(N, src_ap, t0):
            dma(lambda e: e.dma_start(out=h[:, :, :N],
                                      in_=src_ap.rearrange("(kc p) t -> p kc t", p=128)[:, :, t0:t0 + N]),
                writes=["h%d" % c for c in range(KC)], sem="ldh")

        def mixer_in(N, blk, sample):
            norm(N, "gm", KC, h, "h", hn, "hn", D)
            TS = min(128, N)
            nsub = N // TS
            t0 = 0 if sample else blk * NT
            for s in range(nsub):
                b = nb()
                lsl = s % 2
                for kc in range(KC):
                    mm(psum[:TS, b, 0:8], hn[:, kc, s * TS:(s + 1) * TS], wfb[:, kc, :], kc == 0, kc == KC - 1,
                       ["hn%d" % kc, "wfb"], b)
                S.add("dve", lambda e, b=b: e.tensor_tensor(out=lfe[:TS], in0=psum[:TS, b, 0:8], in1=pcol("bf", 0, 8)[:TS],
                                                            op=ALU.add), reads=["ps%d" % b, "prm"], writes=["lfe"])
                S.add("act", lambda e: e.activation(out=lfe[:TS], in_=lfe[:TS], func=AF.Exp, scale=-1.0),
                      reads=["lfe"], writes=["lfe"])
                S.add("act", lambda e: e.activation(out=lfe[:TS], in_=lfe[:TS], func=AF.Ln, bias=1.0),
                      reads=["lfe"], writes=["lfe"])
                S.add("dve", lambda e, lsl=lsl: e.tensor_scalar(out=lft[:TS, lsl, :], in0=lfe[:TS], scalar1=-1.0,
                                                                 scalar2=None, op0=ALU.mult),
                      reads=["lfe"], writes=["lft%d" % lsl])
                if sample:
                    dma(lambda e, lsl=lsl: e.dma_start(out=lfso[:, :], in_=lft[:TS, lsl, :]),
                        reads=["lft%d" % lsl], sem="olf%d" % lsl)
                    S.add("dve", lambda e, lsl=lsl: e.tensor_copy(out=lfS[:TS], in_=lft[:TS, lsl, :]),
                          reads=["lft%d" % lsl], writes=["lfS"])
                else:
                    sc = blk * 4 + s
                    dma(lambda e, sc=sc, lsl=lsl: e.dma_start(out=lfo[sc * 128:(sc + 1) * 128, :], in_=lft[:, lsl, :]),
                        reads=["lft%d" % lsl], sem="olf%d" % lsl)
                    b2 = nb()
                    mm(psum[:, b2, 0:8], pcol("tri", 0, 128), lft[:, lsl, :], True, False, ["lft%d" % lsl, "prm"], b2)
                    mm(psum[:, b2, 0:8], pcol("last", 0, 128), Ftm[:, sc, :], False, True, ["Ftm%d" % sc, "prm"], b2)
                    S.add("dve", lambda e, b2=b2, sc=sc: e.tensor_copy(out=Ftm[:, sc + 1, :], in_=psum[:, b2, 0:8]),
                          reads=["ps%d" % b2], writes=["Ftm%d" % (sc + 1)])
                    S.add("act", lambda e, b2=b2, sc=sc: e.activation(out=negF[:, sc, :], in_=psum[:, b2, 0:8],
                                                                       func=AF.Copy, scale=-1.0),
                          reads=["ps%d" % b2], writes=["negF%d" % sc])
            if not sample:
                b = nb()
                for s in range(4):
                    sc = blk * 4 + s
                    S.add("pe", lambda e, b=b, s=s, sc=sc: e.transpose(psum[0:8, b, s * 128:(s + 1) * 128],
                                                                        Ftm[:, sc + 1, :], pcol("ident", 0, 128)),
                          reads=["Ftm%d" % (sc + 1), "prm"], writes=["ps%d" % b])
                S.add("dve", lambda e, b=b: e.tensor_scalar(out=Fx[:], in0=psum[0:8, b, :], scalar1=1.0 / SCALE,
                                                            scalar2=None, op0=ALU.mult), reads=["ps%d" % b], writes=["Fx"])
                S.add("dve", lambda e: e.tensor_copy(out=Fhi[:, 0, :], in_=Fx[:]), reads=["Fx"], writes=["Fhi0"])
                S.add("dve", lambda e: e.tensor_tensor(out=Fr[:], in0=Fx[:], in1=Fhi[:, 0, :], op=ALU.subtract),
                      reads=["Fx", "Fhi0"], writes=["Fr"])
                S.add("dve", lambda e: e.tensor_copy(out=Fhi[:, 1, :], in_=Fr[:]), reads=["Fr"], writes=["Fhi1"])
                S.add("dve", lambda e: e.tensor_tensor(out=Fx[:], in0=Fr[:], in1=Fhi[:, 1, :], op=ALU.subtract),
                      reads=["Fr", "Fhi1"], writes=["Fx"])
                S.add("dve", lambda e: e.tensor_copy(out=Fhi[:, 2, :], in_=Fx[:]), reads=["Fx"], writes=["Fhi2"])
            if sample:
                ub, L = ubS, 4
            else:
                ub, L = ubP, NT
            for i in range(8):
                us = take(6)
                bcc = fm_proj(N, us[0:2], hn, "hn")
                S.add("act", lambda e, b=bcc: e.activation(out=ccs[:, :N], in_=psum[:, b, :N], func=AF.Copy),
                      reads=["ps%d" % bcc], writes=["ccs"])
                bch = fm_proj(N, us[2:4], hn, "hn")
                if sample:
                    S.add("dve", lambda e, i=i: e.tensor_copy(
                        out=ub[:, :, 0:2], in_=sth[:, i * 8:(i + 1) * 8].rearrange("p (b j) -> p b j", j=2)),
                        reads=["sth", "ub"], writes=["ub"])
                else:
                    S.add("dve", lambda e, i=i: e.tensor_copy(out=ub[:, 0, 0:2], in_=uhist[:, i, :]),
                          reads=["uhist", "ub"], writes=["ub"])
                S.add("dve", lambda e, b=bch: e.tensor_tensor(
                    out=ub[:, :, 2:L + 2], in0=ccs[:, :N].rearrange("p (b l) -> p b l", l=L),
                    in1=psum[:, b, :N].rearrange("p (b l) -> p b l", l=L), op=ALU.mult),
                    reads=["ccs", "ps%d" % bch, "ub"], writes=["ub"])
                bcb = fm_proj(N, us[4:6], hn, "hn")
                S.add("act", lambda e, b=bcb: e.activation(out=cbs[:, :N], in_=psum[:, b, :N], func=AF.Copy),
                      reads=["ps%d" % bcb], writes=["cbs"])
                y13 = y1[:, :N].rearrange("p (b l) -> p b l", l=L)
                S.add("dve", lambda e, i=i, y13=y13: e.tensor_scalar(out=y13, in0=ub[:, :, 0:L], scalar1=pcol("cw", i),
                                                                     scalar2=None, op0=ALU.mult),
                      reads=["ub", "prm"], writes=["y1"])
                S.add("dve", lambda e, i=i, y13=y13: e.scalar_tensor_tensor(
                    out=y13, in0=ub[:, :, 1:L + 1], scalar=pcol("cw", 8 + i), in1=y13, op0=ALU.mult, op1=ALU.add),
                    reads=["ub", "y1", "prm"], writes=["y1"])
                S.add("dve", lambda e, i=i, y13=y13: e.scalar_tensor_tensor(
                    out=y13, in0=ub[:, :, 2:L + 2], scalar=pcol("cw", 16 + i), in1=y13, op0=ALU.mult, op1=ALU.add),
                    reads=["ub", "y1", "prm"], writes=["y1"])
                S.add("dve", lambda e, i=i: e.tensor_tensor(out=zb[:, i, :N], in0=y1[:, :N], in1=cbs[:, :N], op=ALU.mult),
                      reads=["y1", "cbs"], writes=["zb%d" % i])
                if sample:
                    S.add("dve", lambda e, i=i: e.tensor_copy(
                        out=cvs[:, i * 8:(i + 1) * 8].rearrange("p (b j) -> p b j", j=2), in_=ub[:, :, L:L + 2]),
                        reads=["ub"], writes=["cvs"])
                else:
                    S.add("dve", lambda e, i=i: e.tensor_copy(out=uhist[:, i, :], in_=ub[:, 0, L:L + 2]),
                          reads=["ub"], writes=["uhist"])
            if sample:
                dma(lambda e: e.dma_start(out=cvso[:], in_=cvs[:]), reads=["cvs"], sem="ocvs")
            elif blk == NBLK - 1:
                dma(lambda e: e.dma_start(out=cvo[:], in_=uhist[:].rearrange("p a b -> p (a b)")),
                    reads=["uhist"], sem="ocv")
            norm(N, "gc", 8, zb, "zb", zb, "zb", CONV)
            for hh in range(NH):
                us = take(2)
                bk = fm_proj(N, us, hn, "hn")
                ksl = hh % 2
                S.add("act", lambda e, b=bk, ksl=ksl: e.activation(out=kst[:, ksl, :N], in_=psum[:, b, :N], func=AF.Copy),
                      reads=["ps%d" % bk], writes=["kst%d" % ksl])
                dst = ksT if sample else kT
                dma(lambda e, hh=hh, ksl=ksl, dst=dst: e.dma_start(
                    out=dst[hh * 128:(hh + 1) * 128, t0:t0 + N], in_=kst[:, ksl, :N]),
                    reads=["kst%d" % ksl], writes=[] if sample else ["kTd_%d_%d" % (hh, blk)], sem="ok%d" % ksl)
                if sample:
                    S.add("dve", lambda e, hh=hh, ksl=ksl: e.tensor_copy(out=knT[:, hh, :], in_=kst[:, ksl, :N]),
                          reads=["kst%d" % ksl], writes=["knT"])
            for cb in range(2):
                banks = [nb() for _ in range(nsub)]
                for u in range(8):
                    us = take(1)
                    for s in range(nsub):
                        for kl in range(2):
                            kc = 2 * u + kl
                            mm(psum[:TS, banks[s], :], hn[:, kc, s * TS:(s + 1) * TS],
                               ring[:, us[0], kl * 512:(kl + 1) * 512], kc == 0, kc == KC - 1,
                               ["hn%d" % kc, "wr%d" % us[0]], banks[s])
                for s in range(nsub):
                    vsl = s % 2
                    S.add("act", lambda e, b=banks[s], vsl=vsl: e.activation(
                        out=vst[:TS, vsl, :], in_=psum[:TS, b, :], func=AF.Copy),
                        reads=["ps%d" % banks[s]], writes=["vst%d" % vsl])
                    dst = vso if sample else vo
                    r0 = t0 + s * TS
                    dma(lambda e, vsl=vsl, cb=cb, dst=dst, r0=r0: e.dma_start(
                        out=dst[r0:r0 + TS, cb * 512:(cb + 1) * 512], in_=vst[:TS, vsl, :]),
                        reads=["vst%d" % vsl], writes=[] if sample else ["vod_%d_%d" % (blk * 4 + s, cb)],
                        sem="ov%d" % vsl)
                    if sample:
                        S.add("dve", lambda e, vsl=vsl, cb=cb: e.tensor_copy(out=vnw[:, cb * 512:(cb + 1) * 512],
                                                                             in_=vst[:TS, vsl, :]),
                              reads=["vst%d" % vsl], writes=["vnw"])

        kvctr = [0]
        pctr = [0]

        def attention_prompt(blk):
            N = NT
            nsc = 4 * blk + 4
            for hh in range(NH):
                us = take(2)
                bq = fm_proj(N, us, hn, "hn")
                qsl = hh % 2
                S.add("act", lambda e, b=bq, qsl=qsl: e.activation(out=qT[:, qsl, :], in_=psum[:, b, :], func=AF.Copy),
                      reads=["ps%d" % bq], writes=["qT%d" % qsl])
                bo, bd = accpair()
                for sc in range(nsc):
                    diag = sc >= 4 * blk
                    cs = (sc - 4 * blk) * 128 if diag else 0
                    slot = kvctr[0] % 4
                    kvctr[0] += 1
                    dma(lambda e, slot=slot, hh=hh, sc=sc: e.dma_start(
                        out=kvk[:, slot, :], in_=kT[hh * 128:(hh + 1) * 128, sc * 128:(sc + 1) * 128]),
                        reads=["kTd_%d_%d" % (hh, sc // 4)], writes=["kvk%d" % slot], sem="kvk%d" % slot)
                    dma(lambda e, slot=slot, hh=hh, sc=sc: e.dma_start(
                        out=kvv[:, slot, :], in_=vo[sc * 128:(sc + 1) * 128, hh * 128:(hh + 1) * 128]),
                        reads=["vod_%d_%d" % (sc, hh // 4)], writes=["kvv%d" % slot], sem="kvv%d" % slot)
                    bs = nb()
                    mm(psum[:, bs, cs:N], kvk[:, slot, :], qT[:, qsl, cs:N], True, False,
                       ["kvk%d" % slot, "qT%d" % qsl], bs)
                    for part in range(3):
                        mm(psum[:, bs, cs:N], selb[:, hh * 128:(hh + 1) * 128], Fhi[:, part, cs:N], False,
                           (part == 2 and not diag), ["selb", "Fhi%d" % part], bs)
                    if diag:
                        mm(psum[:, bs, cs:cs + 128], identb[:], maskb[:], False, True, ["identb", "maskb"], bs)
                    psl = pctr[0] % 3
                    pctr[0] += 1
                    S.add("act", lambda e, bs=bs, psl=psl, cs=cs, sc=sc, hh=hh: e.activation(
                        out=pTt[:, psl, cs:N], in_=psum[:, bs, cs:N], func=AF.Exp, bias=negF[:, sc, hh:hh + 1],
                        scale=SCALE), reads=["ps%d" % bs, "negF%d" % sc], writes=["pT%d" % psl])
                    mm(psum[:, bo, cs:N], kvv[:, slot, :], pTt[:, psl, cs:N], sc == 0, sc == nsc - 1,
                       ["kvv%d" % slot, "pT%d" % psl], bo)
                    mm(psum[:, bd, cs:N], onesf[:], pTt[:, psl, cs:N], sc == 0, sc == nsc - 1,
                       ["onesf", "pT%d" % psl], bd)
                S.add("dve", lambda e, bd=bd: e.reciprocal(out=rden[:], in_=psum[:, bd, :]),
                      reads=["ps%d" % bd], writes=["rden"])
                S.add("dve", lambda e, bo=bo, hh=hh: e.tensor_tensor(out=zb[:, 8 + hh, :], in0=psum[:, bo, :],
                                                                     in1=rden[:], op=ALU.mult),
                      reads=["ps%d" % bo, "rden"], writes=["zb%d" % (8 + hh)])

        def mixer_out(N):
            norm(N, "ga", 8, zb, "zb", zb, "zb", CONV, c0=8)
            for dc in range(KC):
                us = take(2)
                b = fm_proj(N, us, zb, "zb")
                S.add("dve", lambda e, b=b, dc=dc: e.tensor_tensor(out=h[:, dc, :N], in0=psum[:, b, :N],
                                                                   in1=h[:, dc, :N], op=ALU.add),
                      reads=["ps%d" % b, "h%d" % dc], writes=["h%d" % dc])

        def ple(N, p_ap, t0):
            norm(N, "gp", KC, h, "h", hn, "hn", D)
            dma(lambda e: e.dma_start(out=kst[:, :, :N], in_=p_ap.rearrange("(kc p) t -> p kc t", p=128)[:, :, t0:t0 + N]),
                writes=["kst0", "kst1"], sem="ldp")
            S.add("dve", lambda e: e.tensor_copy(out=ptb[:, :, :N], in_=kst[:, :, :N]), reads=["kst0", "kst1"],
                  writes=["ptb"])
            for q in range(4):
                pu = take(1)[0]
                pbanks = []
                for dcl in range(4):
                    b = 4 + dcl
                    pbanks.append(b)
                    for kc in range(2):
                        mm(psum[:, b, :N], ring[:, pu, (dcl * 2 + kc) * 128:(dcl * 2 + kc + 1) * 128], ptb[:, kc, :N],
                           kc == 0, kc == 1, ["wr%d" % pu, "ptb"], b)
                for dcl in range(4):
                    dc = 4 * q + dcl
                    us = take(2)
                    bgt = fm_proj(N, us, hn, "hn")
                    S.add("act", lambda e, b=bgt: e.activation(out=ccs[:, :N], in_=psum[:, b, :N], func=AF.Sigmoid),
                          reads=["ps%d" % bgt], writes=["ccs"])
                    pb = pbanks[dcl]
                    S.add("dve", lambda e, pb=pb: e.tensor_tensor(out=y1[:, :N], in0=ccs[:, :N], in1=psum[:, pb, :N],
                                                                  op=ALU.mult), reads=["ccs", "ps%d" % pb], writes=["y1"])
                    S.add("dve", lambda e, dc=dc: e.tensor_tensor(out=h[:, dc, :N], in0=h[:, dc, :N], in1=y1[:, :N],
                                                                  op=ALU.add), reads=["y1", "h%d" % dc], writes=["h%d" % dc])

        def final_out(N, dst, t0):
            sumsq_rstd(N, KC, h, "h", D)
            for c in range(KC):
                sl = c % 2
                S.add("dve", lambda e, c=c, sl=sl: e.scalar_tensor_tensor(
                    out=vst[:, sl, :N], in0=h[:, c, :N], scalar=pcol("gf", c), in1=rstd[:, :N],
                    op0=ALU.mult, op1=ALU.mult), reads=["h%d" % c, "rstd", "prm"], writes=["vst%d" % sl])
                dma(lambda e, c=c, sl=sl: e.dma_start(out=dst[c * 128:(c + 1) * 128, t0:t0 + N], in_=vst[:, sl, :N]),
                    reads=["vst%d" % sl], sem="ov%d" % sl)

        def attention_sample():
            N = NS
            for hh in range(NH):
                us = take(2)
                b = fm_proj(N, us, hn, "hn")
                S.add("act", lambda e, b=b, hh=hh: e.activation(out=qTs[:, hh, :], in_=psum[:, b, :N], func=AF.Copy),
                      reads=["ps%d" % b], writes=["qTs"])
            b = nb()
            btri16 = prm[0:NS, PC["btri"][0]:PC["btri"][0] + NS]
            mm(psum[:NS, b, 0:8], btri16, lfS[:NS, :], True, True, ["lfS", "prm"], b)
            S.add("act", lambda e, b=b: e.activation(out=negE[:], in_=psum[:NS, b, 0:8], func=AF.Copy, scale=-1.0),
                  reads=["ps%d" % b], writes=["negE"])
            bo, bd = accpair()
            cctr = 0
            for bi in range(4):
                ib = bi % 2
                dma(lambda e, bi=bi, ib=ib: e.dma_start(out=ptbc[:, ib, :], in_=ptab[bi:bi + 1, :].partition_broadcast(128)),
                    writes=["ptbc%d" % ib], sem="ptb%d" % ib)
                dma(lambda e, bi=bi, ib=ib: e.dma_start(out=pcolb[:, ib:ib + 1],
                                                        in_=ptab[bi:bi + 1, :].rearrange("a (p f) -> p (a f)", f=1)),
                    writes=["pcolb%d" % ib], sem="ptc%d" % ib)
                S.add("dve", lambda e, ib=ib: e.tensor_scalar(out=idx0f[:], in0=ptbc[:, ib, :], scalar1=128.0,
                                                               scalar2=pcol("iota", 0, 1), op0=ALU.mult, op1=ALU.add),
                      reads=["ptbc%d" % ib, "prm"], writes=["idx0f"])
                for hh in range(NH):
                    S.add("dve", lambda e, hh=hh: e.tensor_scalar(out=idxh[:, hh, :], in0=idx0f[:],
                                                                   scalar1=float(hh * NPOOL * 128), scalar2=None,
                                                                   op0=ALU.add),
                          reads=["idx0f"], writes=["idxh%d" % hh])
                    S.add("dve", lambda e, hh=hh, ib=ib: e.tensor_scalar(out=pcolh[:, hh:hh + 1], in0=pcolb[:, ib:ib + 1],
                                                                          scalar1=float(hh * NPOOL), scalar2=None,
                                                                          op0=ALU.add),
                          reads=["pcolb%d" % ib], writes=["pcolh%d" % hh])
                for hh in range(NH):
                    col0 = (bi * NH + hh) * 4
                    dma(lambda e, hh=hh: e.indirect_dma_start(
                        out=Lb[:, :], out_offset=None, in_=clf[:, :],
                        in_offset=bass.IndirectOffsetOnAxis(ap=pcolh[:, hh:hh + 1], axis=0)),
                        reads=["pcolh%d" % hh], writes=["Lb"], sem="lb", eng="pool")
                    b = nb()
                    S.add("pe", lambda e, b=b: e.transpose(psum[:, b, 0:64], Lb[:, :],
                                                            prm[0:64, PC["ident"][0]:PC["ident"][0] + 64]),
                          reads=["Lb", "prm"], writes=["ps%d" % b])
                    S.add("dve", lambda e, b=b: e.tensor_copy(out=LT[:], in_=psum[:, b, 0:64]),
                          reads=["ps%d" % b], writes=["LT"])
                    ba, bt = nb(), nb()
                    mm(psum[:, ba, 0:64], pcol("ustr", 0, 128), LT[:], True, True, ["LT", "prm"], ba)
                    mm(psum[:, bt, 0:64], onesf[:], LT[:], True, True, ["LT", "onesf"], bt)
                    S.add("dve", lambda e, bt=bt: e.tensor_copy(out=Dm[:], in_=psum[:, bt, 0:64]),
                          reads=["ps%d" % bt], writes=["Dm"])
                    S.add("dve", lambda e: e.tensor_tensor_scan(out=Csc[:], data0=ones64[:], data1=Dm[:], initial=0.0,
                                                                op0=ALU.mult, op1=ALU.add),
                          reads=["Dm", "ones64"], writes=["Csc"])
                    S.add("dve", lambda e, ba=ba: e.scalar_tensor_tensor(out=Dm[:], in0=Csc[:], scalar=-1.0,
                                                                         in1=psum[:, ba, 0:64], op0=ALU.mult, op1=ALU.add),
                          reads=["Csc", "ps%d" % ba, "Dm"], writes=["Dm"])
                    S.add("dve", lambda e: e.tensor_scalar(out=Dm[:], in0=Dm[:], scalar1=Csc[:, 63:64], scalar2=None,
                                                           op0=ALU.add), reads=["Dm", "Csc"], writes=["Dm"])
                    for ch in range(NPG // PCH):
                        sl = cctr % 2
                        cctr += 1
                        for pp in range(PCH):
                            j = ch * PCH + pp
                            dma(lambda e, hh=hh, j=j, sl=sl, pp=pp: e.indirect_dma_start(
                                out=kvp[:, sl, pp, :], out_offset=None, in_=ckv[:, :],
                                in_offset=bass.IndirectOffsetOnAxis(ap=idxh[:, hh, j:j + 1], axis=0)),
                                reads=["idxh%d" % hh], writes=["kvp%d_%d" % (sl, pp)], sem="pg%d_%d" % (sl, pp), eng="pool")
                        bs = nb()
                        for pp in range(PCH):
                            mm(psum[:, bs, pp * 4:(pp + 1) * 4], kvp[:, sl, pp, 0:128],
                               qTs[:, hh, bi * 4:(bi + 1) * 4], True, True, ["kvp%d_%d" % (sl, pp), "qTs"], bs)
                        S.add("dve", lambda e, bs=bs, ch=ch: e.scalar_tensor_tensor(
                            out=tmpS[:].rearrange("p (a t) -> p a t", t=4),
                            in0=psum[:, bs, 0:PCH * 4].rearrange("p (a t) -> p a t", t=4), scalar=SCALE,
                            in1=Dm[:, ch * PCH:(ch + 1) * PCH].unsqueeze(2).to_broadcast([128, PCH, 4]),
                            op0=ALU.mult, op1=ALU.add), reads=["ps%d" % bs, "Dm"], writes=["tmpS"])
                        S.add("act", lambda e, sl=sl: e.activation(out=Pm[:, sl, :], in_=tmpS[:], func=AF.Exp),
                              reads=["tmpS"], writes=["Pm%d" % sl])
                        for pp in range(PCH):
                            first = (ch == 0 and pp == 0)
                            mm(psum[:, bo, col0:col0 + 4], kvp[:, sl, pp, 128:256],
                               Pm[:, sl, pp * 4:(pp + 1) * 4], first, False, ["kvp%d_%d" % (sl, pp), "Pm%d" % sl], bo)
                            mm(psum[:, bd, col0:col0 + 4], onesf[:], Pm[:, sl, pp * 4:(pp + 1) * 4], first, False,
                               ["onesf", "Pm%d" % sl], bd)
                    bs = nb()
                    mm(psum[:NS, bs, 0:4], knT[:, hh, :], qTs[:, hh, bi * 4:(bi + 1) * 4], True, True, ["knT", "qTs"], bs)
                    S.add("dve", lambda e, bs=bs, bi=bi: e.scalar_tensor_tensor(
                        out=tmpS[:NS, 0:4], in0=psum[:NS, bs, 0:4], scalar=SCALE, in1=pcol("maskS", bi * 4, 4)[:NS],
                        op0=ALU.mult, op1=ALU.add), reads=["ps%d" % bs, "prm"], writes=["tmpS"])
                    sl = cctr % 2
                    cctr += 1
                    S.add("act", lambda e, sl=sl, hh=hh: e.activation(out=Pm[:NS, sl, 0:4], in_=tmpS[:NS, 0:4], func=AF.Exp,
                                                                       bias=negE[:, hh:hh + 1]),
                          reads=["tmpS", "negE"], writes=["Pm%d" % sl])
                    mm(psum[:, bo, col0:col0 + 4], vnw[:, hh * 128:(hh + 1) * 128], Pm[:NS, sl, 0:4], False, True,
                       ["vnw", "Pm%d" % sl], bo)
                    mm(psum[:, bd, col0:col0 + 4], onesf[:NS, :], Pm[:NS, sl, 0:4], False, True, ["onesf", "Pm%d" % sl], bd)
            S.add("dve", lambda e: e.reciprocal(out=rden[:, :128], in_=psum[:, bd, :128]), reads=["ps%d" % bd], writes=["rden"])
            S.add("dve", lambda e: e.tensor_tensor(out=osb[:], in0=psum[:, bo, :128], in1=rden[:, :128], op=ALU.mult),
                  reads=["ps%d" % bo, "rden"], writes=["osb"])
            for hh in range(NH):
                S.add("dve", lambda e, hh=hh: e.tensor_copy(
                    out=zb[:, 8 + hh, 0:NS].rearrange("p (i t) -> p i t", t=4),
                    in_=osb[:].rearrange("p (i h t) -> p i h t", h=NH, t=4)[:, :, hh, :]),
                    reads=["osb"], writes=["zb%d" % (8 + hh)])

        for blk in range(NBLK):
            load_block(NT, xT, blk * NT)
            ffn(NT, "g1")
            mixer_in(NT, blk, False)
            attention_prompt(blk)
            mixer_out(NT)
            ffn(NT, "g2")
            ple(NT, pT, blk * NT)
            final_out(NT, yT, blk * NT)
        if WITH_SAMPLE:
            load_block(NS, xsT, 0)
            ffn(NS, "g1")
            mixer_in(NS, None, True)
            attention_sample()
            mixer_out(NS)
            ffn(NS, "g2")
            ple(NS, psT, 0)
            final_out(NS, ysT, 0)
            assert wstate["next_use"] == len(plan), (wstate["next_use"], len(plan))

        finals = [n for n in dsem if n[0] == "o"]
        S.emit(nc, None, esem, dsem, finals)
    return nc


_CACHE = {}


def kernel(**inp):
    inp = {k: np.asarray(v) for k, v in inp.items()}
    if "nc" not in _CACHE:
        _CACHE["nc"] = build_program()
    nc = _CACHE["nc"]
    wf = np.ascontiguousarray(inp["w_in"][0][:, 6144:6152].reshape(16, 128, 8).transpose(1, 0, 2).reshape(128, 128))
    base = pack_weights(inp, 0)
    if WITH_SAMPLE:
        ck = inp["cache_k"][0].transpose(2, 0, 3, 1)
        cv = inp["cache_v"][0].transpose(2, 0, 1, 3)
        ckv = np.concatenate([ck, cv], axis=3).reshape(NH * NPOOL * 128, 256)
        clf = np.ascontiguousarray(inp["cache_logf"][0].transpose(2, 0, 1)).reshape(NH * NPOOL, 128)
    in_maps = []
    for c in range(8):
        m = {
            "xT": np.ascontiguousarray(inp["x_prompt"][c].T),
            "pT": np.ascontiguousarray(inp["p_prompt"][0, c].T),
            "xsT": np.ascontiguousarray(inp["x_sample"][4 * c:4 * c + 4].reshape(NS, D).T),
            "psT": np.ascontiguousarray(inp["p_sample"][0, 4 * c:4 * c + 4].reshape(NS, PLE).T),
            "stc": np.ascontiguousarray(inp["state_conv"][0, 4 * c:4 * c + 4].reshape(4, 2, 8, 128)
                                        .transpose(3, 2, 0, 1)).reshape(128, 64),
            "wst": base, "par": pack_params(inp, c), "wfd": wf,
        }
        if WITH_SAMPLE:
            m["ckv"] = ckv
            m["clf"] = clf
            m["ptab"] = np.ascontiguousarray(inp["page_table"][4 * c:4 * c + 4].astype(np.int32))
        in_maps.append(m)
    res = run_bass_kernel_spmd(nc, in_maps, core_ids=list(range(8))).results
    y_p = np.stack([res[c]["yT"].T for c in range(8)], 0)
    k_p = np.stack([res[c]["kT"].T.reshape(SEQ, NH, HD) for c in range(8)], 0)[None]
    v_p = np.stack([res[c]["vo"].reshape(SEQ, NH, HD) for c in range(8)], 0)[None]
    lf_p = np.stack([res[c]["lfo"] for c in range(8)], 0)[None]
    cv_p = np.stack([res[c]["cvo"].reshape(128, 8, 2).transpose(2, 1, 0).reshape(2, CONV) for c in range(8)], 0)[None]

    def cat(name, fn):
        return np.concatenate([fn(res[c][name]) for c in range(8)], 0)
    y_s = cat("ysT", lambda a: a.T.reshape(4, 4, D))
    k_s = cat("ksT", lambda a: a.T.reshape(4, 4, NH, HD))[None]
    v_s = cat("vso", lambda a: a.reshape(4, 4, NH, HD))[None]
    lf_s = cat("lfso", lambda a: a.reshape(4, 4, NH))[None]
    cv_s = cat("cvso", lambda a: a.reshape(128, 8, 4, 2).transpose(2, 3, 1, 0).reshape(4, 2, CONV))[None]
    f = np.float32
    return tuple(np.ascontiguousarray(a, f) for a in (y_p, y_s, k_p, v_p, lf_p, cv_p, k_s, v_s, lf_s, cv_s))
```

```python
import contextlib
import numpy as np
import concourse.bass as bass
import concourse.mybir as mybir
from concourse.bass_utils import run_bass_kernel_spmd

F32 = mybir.dt.float32
BF16 = mybir.dt.bfloat16
AF = mybir.ActivationFunctionType
ALU = mybir.AluOpType

D = 2048
KC = 16
DFF = 5632
NFC = 44
G = 4
NG = NFC // G
CONV = 1024
NH = 8
HD = 128
PLE = 256
SEQ = 2048
NBLK = 4
NT = 512
WITH_SAMPLE = True
NS = 16
EPS = 1e-6
SCALE = HD ** -0.5
RING = 8
NSTG = 3
LOOK = 10


def _fm_units(W, col0):
    blk = W[:, col0:col0 + 128].reshape(16, 128, 128)
    return [np.ascontiguousarray(blk[half * 8:(half + 1) * 8].transpose(1, 0, 2)).reshape(128, 1024)
            for half in range(2)]


def pack_weights(inp, core):
    units = []

    def ffn(wg, wu, wd):
        for g in range(NG):
            for j in range(G):
                fc = g * G + j
                units.extend(_fm_units(wg, fc * 128))
                units.extend(_fm_units(wu, fc * 128))
            for half in range(2):
                for j in range(G):
                    fc = g * G + j
                    units.append(np.ascontiguousarray(wd[fc * 128:(fc + 1) * 128, half * 1024:(half + 1) * 1024]))

    ffn(inp["w_ffn1_gate"][0], inp["w_ffn1_up"][0], inp["w_ffn1_down"][0])
    win = inp["w_in"][0]
    for i in range(8):
        units.extend(_fm_units(win, 1024 + 128 * i))
        units.extend(_fm_units(win, 2048 + 128 * i))
        units.extend(_fm_units(win, 128 * i))
    for h in range(NH):
        units.extend(_fm_units(win, 4096 + 128 * h))
    for cb in range(2):
        col0 = 5120 + 512 * cb
        blk = win[:, col0:col0 + 512].reshape(16, 128, 512)
        for u in range(8):
            units.append(np.ascontiguousarray(blk[2 * u:2 * u + 2].transpose(1, 0, 2)).reshape(128, 1024))
    for h in range(NH):
        units.extend(_fm_units(win, 3072 + 128 * h))
    wo = inp["w_out"][0]
    for dc in range(16):
        units.extend(_fm_units(wo, dc * 128))
    ffn(inp["w_ffn2_gate"][0], inp["w_ffn2_up"][0], inp["w_ffn2_down"][0])
    wpg = inp["w_ple_gate"][0]
    wpp = inp["w_ple_proj"][0]
    for q in range(4):
        blk = wpp[:, q * 512:(q + 1) * 512].reshape(2, 128, 4, 128)
        units.append(np.ascontiguousarray(blk.transpose(1, 2, 0, 3)).reshape(128, 1024))
        for dcl in range(4):
            units.extend(_fm_units(wpg, (4 * q + dcl) * 128))
    assert len(units) == NU, len(units)
    return np.stack(units, axis=0)


NU_FFN = NG * (G * 4 + G * 2)
NU_IN = 8 * 6 + NH * 2 + 16
NU = 2 * NU_FFN + NU_IN + NH * 2 + 32 + 4 + 32
NX = 6


def pack_extras(inp, core):
    win = inp["w_in"][0]
    return np.stack(_fm_units(win, 3072 + 128 * core) + _fm_units(win, 4096 + 128 * core) +
                    _fm_units(win, 5120 + 128 * core), axis=0)


PC = {}
_o = 0
for _n, _w in [("g1", 16), ("gm", 16), ("g2", 16), ("gp", 16), ("gf", 16), ("gc", 8), ("ga", 8), ("cw", 24),
               ("bf", 8), ("hsel", 8), ("iota", 1), ("ident", 128), ("tri", 128), ("last", 128), ("mask", 128), ("ustr", 128),
               ("maskS", 128), ("btri", 128), ("sel", 1024)]:
    PC[_n] = (_o, _w)
    _o += _w
NPAR = _o


def pack_params(inp, core):
    P = np.zeros((128, NPAR), np.float32)

    def put(name, arr):
        o, w = PC[name]
        P[:, o:o + w] = arr

    put("g1", inp["norm_ffn1"][0].reshape(16, 128).T)
    put("gm", inp["norm_mix"][0].reshape(16, 128).T)
    put("g2", inp["norm_ffn2"][0].reshape(16, 128).T)
    put("gp", inp["norm_ple"][0].reshape(16, 128).T)
    put("gf", inp["norm_final"].reshape(16, 128).T)
    put("gc", inp["norm_conv_out"][0].reshape(8, 128).T)
    put("ga", inp["norm_attn_out"][0].reshape(8, 128).T)
    put("cw", inp["conv_w"][0].reshape(3, 8, 128).transpose(2, 0, 1).reshape(128, 24))
    put("bf", np.broadcast_to(inp["b_f"][0][None, :], (128, 8)))
    hs = np.zeros((128, 8), np.float32)
    hs[:, core] = 1.0
    put("hsel", hs)
    put("iota", np.arange(128, dtype=np.float32)[:, None])
    put("ident", np.eye(128, dtype=np.float32))
    ar = np.arange(128)
    put("tri", (ar[:, None] <= ar[None, :]).astype(np.float32))
    last = np.zeros((128, 128), np.float32)
    last[127, :] = 1.0
    put("last", last)
    put("mask", np.where(ar[:, None] <= ar[None, :], 0.0, -30000.0).astype(np.float32))
    put("ustr", (ar[:, None] > ar[None, :]).astype(np.float32))
    same = (ar[:, None] // 4) == (ar[None, :] // 4)
    put("maskS", np.where(same & (ar[:, None] <= ar[None, :]), 0.0, -30000.0).astype(np.float32))
    put("btri", (same & (ar[:, None] <= ar[None, :])).astype(np.float32))
    sel = np.zeros((128, 8, 128), np.float32)
    for h in range(8):
        sel[h, h, :] = 1.0
    put("sel", sel.reshape(128, 1024))
    return P


class Sched:
    def __init__(self):
        self.ops = {e: [] for e in ("pe", "act", "dve", "pool", "sp")}
        self.regions = {}
        self.dma_cnt = {}

    def add(self, eng, fn, reads=(), writes=(), dma=None):
        deps = []
        for r in reads:
            reg = self.regions.get(r)
            if reg is not None and reg["w"] is not None:
                deps.append(reg["w"])
        for w in writes:
            reg = self.regions.get(w)
            if reg is not None:
                if reg["w"] is not None:
                    deps.append(reg["w"])
                deps.extend(reg["r"].values())
        idx = len(self.ops[eng])
        if dma is not None:
            k = self.dma_cnt.get(dma, 0) + 1
            self.dma_cnt[dma] = k
            ref = ("dma", dma, k)
            key = "dma:" + dma
        else:
            ref = ("eng", eng, idx)
            key = eng
        if eng == "pe":
            deps = [d for d in deps if not (d[0] == "eng" and d[1] == "pe")]
        self.ops[eng].append({"fn": fn, "deps": deps, "marked": False, "dma": dma})
        for r in reads:
            reg = self.regions.setdefault(r, {"w": None, "r": {}})
            reg["r"][key] = ref
        for w in writes:
            self.regions[w] = {"w": ref, "r": {}}
        return ref

    def emit(self, nc, engines, esem, dsem, final_waits):
        for e, lst in self.ops.items():
            for op in lst:
                for d in op["deps"]:
                    if d[0] == "eng":
                        self.ops[d[1]][d[2]]["marked"] = True
        cum = {}
        for e, lst in self.ops.items():
            c = 0
            arr = []
            for op in lst:
                if op["marked"]:
                    c += 1
                arr.append(c)
            cum[e] = arr

        def run(e, eng):
            waited = {}
            for i, op in enumerate(self.ops[e]):
                need = {}
                for d in op["deps"]:
                    if d[0] == "eng":
                        if d[1] == e and d[2] >= i:
                            continue
                        s, v = ("e", d[1]), cum[d[1]][d[2]]
                    else:
                        s, v = ("d", d[1]), 16 * d[2]
                    if v > need.get(s, 0):
                        need[s] = v
                for s, v in need.items():
                    if waited.get(s, 0) >= v:
                        continue
                    waited[s] = v
                    sem = esem[s[1]] if s[0] == "e" else dsem[s[1]]
                    eng.wait_ge(sem, v)
                ins = op["fn"](eng)
                if op["dma"] is not None:
                    ins.then_inc(dsem[op["dma"]], 16)
                elif op["marked"]:
                    ins.then_inc(esem[e], 1)
            if e == "sp":
                for name in final_waits:
                    eng.wait_ge(dsem[name], 16 * self.dma_cnt[name])

        with nc.Block() as block:
            @block.sync
            def _(eng):
                run("sp", eng)

            @block.tensor
            def _(eng):
                run("pe", eng)

            @block.scalar
            def _(eng):
                run("act", eng)

            @block.vector
            def _(eng):
                run("dve", eng)

            @block.gpsimd
            def _(eng):
                run("pool", eng)


NPOOL = 2560
NPG = 64
PCH = 4


def build_program():
    nc = bass.Bass("TRN2", target_bir_lowering=False)
    S = Sched()
    dt = nc.dram_tensor
    I32 = mybir.dt.int32
    xT = dt("xT", [D, SEQ], F32, kind="ExternalInput").ap()
    pT = dt("pT", [PLE, SEQ], F32, kind="ExternalInput").ap()
    xsT = dt("xsT", [D, NS], F32, kind="ExternalInput").ap()
    psT = dt("psT", [PLE, NS], F32, kind="ExternalInput").ap()
    stc = dt("stc", [128, 64], F32, kind="ExternalInput").ap()
    wst = dt("wst", [NU, 128, 1024], F32, kind="ExternalInput").ap()
    par = dt("par", [128, NPAR], F32, kind="ExternalInput").ap()
    wfd = dt("wfd", [128, KC * 8], F32, kind="ExternalInput").ap()
    if WITH_SAMPLE:
        ckv = dt("ckv", [NH * NPOOL * 128, 256], F32, kind="ExternalInput").ap()
        clf = dt("clf", [NH * NPOOL, 128], F32, kind="ExternalInput").ap()
        ptab = dt("ptab", [4, NPG], I32, kind="ExternalInput").ap()
    yT = dt("yT", [D, SEQ], F32, kind="ExternalOutput").ap()
    kT = dt("kT", [CONV, SEQ], F32, kind="ExternalOutput").ap()
    vo = dt("vo", [SEQ, CONV], F32, kind="ExternalOutput").ap()
    lfo = dt("lfo", [SEQ, NH], F32, kind="ExternalOutput").ap()
    cvo = dt("cvo", [128, 16], F32, kind="ExternalOutput").ap()
    ysT = dt("ysT", [D, NS], F32, kind="ExternalOutput").ap()
    ksT = dt("ksT", [CONV, NS], F32, kind="ExternalOutput").ap()
    vso = dt("vso", [NS, CONV], F32, kind="ExternalOutput").ap()
    lfso = dt("lfso", [NS, NH], F32, kind="ExternalOutput").ap()
    cvso = dt("cvso", [128, 64], F32, kind="ExternalOutput").ap()

    es = contextlib.ExitStack()
    with es:
        def sb(name, shape, dtype):
            return es.enter_context(nc.sbuf_tensor(name, shape, dtype))

        prm = sb("prm", [128, NPAR], F32)
        onesb = sb("onesb", [128, 128], BF16)
        onesf = sb("onesf", [128, 128], F32)
        identb = sb("identb", [128, 128], BF16)
        maskb = sb("maskb", [128, 128], BF16)
        selb = sb("selb", [8, 1024], BF16)
        epst = sb("epst", [128, 1], F32)
        h = sb("h", [128, KC, NT], F32)
        hn = sb("hn", [128, KC, NT], BF16)
        sqr = sb("sqr", [128, 2, NT], BF16)
        rs = sb("rs", [128, NT], F32)
        rstd = sb("rstd", [128, NT], F32)
        sg = sb("sg", [128, 2, NT], BF16)
        abuf = sb("abuf", [128, G, NT], BF16)
        stg = sb("stg", [128, NSTG, 1024], F32)
        ring = sb("ring", [128, RING, 1024], BF16)
        ccs = sb("ccs", [128, NT], F32)
        cbs = sb("cbs", [128, NT], F32)
        y1 = sb("y1", [128, NT], F32)
        ubP = sb("ubP", [128, 1, NT + 2], F32)
        ubS = sb("ubS", [128, 4, 6], F32)
        uhist = sb("uhist", [128, 8, 2], F32)
        sth = sb("sth", [128, 64], F32)
        cvs = sb("cvs", [128, 64], F32)
        kst = sb("kst", [128, 2, NT], F32)
        vst = sb("vst", [128, 2, NT], F32)
        zb = sb("zb", [128, KC, NT], BF16)
        qT = sb("qT", [128, 2, NT], F32)
        pTt = sb("pTt", [128, 3, NT], F32)
        kvk = sb("kvk", [128, 4, 128], F32)
        kvv = sb("kvv", [128, 4, 128], F32)
        rden = sb("rden", [128, NT], F32)
        lft = sb("lft", [128, 2, 8], F32)
        lfe = sb("lfe", [128, 8], F32)
        lfS = sb("lfS", [128, 8], F32)
        Ftm = sb("Ftm", [128, KC + 1, 8], F32)
        negF = sb("negF", [128, KC, 8], F32)
        Fx = sb("Fx", [8, NT], F32)
        Fr = sb("Fr", [8, NT], F32)
        Fhi = sb("Fhi", [8, 3, NT], BF16)
        wfb = sb("wfb", [128, KC, 8], BF16)
        wfs = sb("wfs", [128, KC, 8], F32)
        ptb = sb("ptb", [128, 2, NT], BF16)
        ptbc = sb("ptbc", [128, 2, NPG], I32)
        idxb = sb("idxb", [128, 2, NPG], I32)
        pcolb = sb("pcolb", [64, 2], I32)
        Lb = sb("Lb", [64, 128], F32)
        LT = sb("LT", [128, 64], F32)
        Dm = sb("Dm", [128, 64], F32)
        Csc = sb("Csc", [128, 64], F32)
        ones64 = sb("ones64", [128, 64], F32)
        kvp = sb("kvp", [128, 2, PCH, 256], F32)
        Pm = sb("Pm", [128, 2, PCH * 4], F32)
        tmpS = sb("tmpS", [128, PCH * 4], F32)
        qTs = sb("qTs", [128, NH, NS], F32)
        knT = sb("knT", [128, NH, NS], F32)
        vnw = sb("vnw", [NS, CONV], F32)
        negE = sb("negE", [NS, 8], F32)
        osb = sb("osb", [128, 128], F32)
        idx0f = sb("idx0f", [128, NPG], F32)
        idxh = sb("idxh", [128, NH, NPG], I32)
        pcolh = sb("pcolh", [64, NH], I32)
        psum = es.enter_context(nc.psum_tensor("psum", [128, 8, 512], F32))

        esem = {e: es.enter_context(nc.semaphore("es_" + e)) for e in ("pe", "act", "dve", "pool", "sp")}
        dsem = {}

        def pcol(name, i=0, w=1):
            o, _ = PC[name]
            return prm[:, o + i:o + i + w]

        bank_ctr = [0]

        def nb():
            b = bank_ctr[0] % 4
            bank_ctr[0] += 1
            return b

        acc_ctr = [0]

        def accpair():
            k = acc_ctr[0] % 2
            acc_ctr[0] += 1
            return 4 + 2 * k, 5 + 2 * k

        def dma(fn, reads=(), writes=(), sem=None, eng="sp"):
            if sem not in dsem:
                dsem[sem] = es.enter_context(nc.semaphore("ds_" + sem))
            return S.add(eng, fn, reads=reads, writes=writes, dma=sem)

        p_full = list(range(NU))
        cut = NU_FFN + NU_IN
        plan = p_full * (NBLK + 1)
        if not WITH_SAMPLE:
            plan = p_full * NBLK
        wstate = {"next_use": 0, "next_fetch": 0}

        def fetch_upto(u_hi):
            while wstate["next_fetch"] < min(u_hi, len(plan)):
                u = wstate["next_fetch"]
                wstate["next_fetch"] += 1
                s4, s16, ui = u % NSTG, u % RING, plan[u]
                srcu = wst[ui]
                dma(lambda e, s4=s4, srcu=srcu: e.dma_start(out=stg[:, s4, :], in_=srcu),
                    writes=["stg%d" % s4], sem="wst%d" % s4)
                if u % 3 == 2:
                    S.add("dve", lambda e, s4=s4, s16=s16: e.tensor_copy(out=ring[:, s16, :], in_=stg[:, s4, :]),
                          reads=["stg%d" % s4], writes=["wr%d" % s16])
                else:
                    S.add("act", lambda e, s4=s4, s16=s16: e.activation(out=ring[:, s16, :], in_=stg[:, s4, :],
                                                                        func=AF.Copy),
                          reads=["stg%d" % s4], writes=["wr%d" % s16])

        def take(n):
            u0 = wstate["next_use"]
            wstate["next_use"] += n
            assert n <= RING
            fetch_upto(max(u0 + n, u0 + RING))
            return [(u % RING) for u in range(u0, u0 + n)]

        dma(lambda e: e.dma_start(out=prm[:], in_=par[:]), writes=["prm"], sem="prm")
        S.add("dve", lambda e: e.tensor_copy(out=identb[:], in_=pcol("ident", 0, 128)), reads=["prm"], writes=["identb"])
        S.add("dve", lambda e: e.tensor_copy(out=maskb[:], in_=pcol("mask", 0, 128)), reads=["prm"], writes=["maskb"])
        S.add("dve", lambda e: e.tensor_copy(out=selb[:], in_=prm[0:8, PC["sel"][0]:PC["sel"][0] + 1024]),
              reads=["prm"], writes=["selb"])
        S.add("dve", lambda e: e.memset(onesb[:], 1.0), writes=["onesb"])
        S.add("dve", lambda e: e.memset(onesf[:], 1.0), writes=["onesf"])
        S.add("dve", lambda e: e.memset(ones64[:], 1.0), writes=["ones64"])
        S.add("dve", lambda e: e.memset(epst[:], EPS), writes=["epst"])
        S.add("dve", lambda e: e.memset(Ftm[:, 0, :], 0.0), writes=["Ftm0"])
        S.add("dve", lambda e: e.memset(uhist[:], 0.0), writes=["uhist"])
        dma(lambda e: e.dma_start(out=wfs[:].rearrange("p a b -> p (a b)"), in_=wfd[:]), writes=["wfs"], sem="wf")
        S.add("dve", lambda e: e.tensor_copy(out=wfb[:], in_=wfs[:]), reads=["wfs"], writes=["wfb"])
        dma(lambda e: e.dma_start(out=sth[:], in_=stc[:]), writes=["sth"], sem="sth")

        def mm(out_ap, lhsT, rhs, start, stop, reads, bank):
            S.add("pe", lambda e: e.matmul(out_ap, lhsT, rhs, start=start, stop=stop),
                  reads=reads, writes=["ps%d" % bank])

        def sumsq_rstd(N, nchunks, src, srcreg, dim):
            b = nb()
            for c in range(nchunks):
                sl = c % 2
                S.add("act", lambda e, c=c, sl=sl: e.activation(out=sqr[:, sl, :N], in_=src[:, c, :N], func=AF.Square),
                      reads=["%s%d" % (srcreg, c)], writes=["sqr%d" % sl])
                mm(psum[:, b, :N], onesb[:], sqr[:, sl, :N], c == 0, c == nchunks - 1, ["sqr%d" % sl, "onesb"], b)
            S.add("act", lambda e: e.activation(out=rs[:, :N], in_=psum[:, b, :N], func=AF.Sqrt, bias=epst[:],
                                                scale=1.0 / dim), reads=["ps%d" % b, "epst"], writes=["rs"])
            S.add("dve", lambda e: e.reciprocal(out=rstd[:, :N], in_=rs[:, :N]), reads=["rs"], writes=["rstd"])

        def norm(N, gname, nchunks, src, srcreg, dst, dstreg, dim, c0=0):
            sv = src[:, c0:c0 + nchunks, :]
            sumsq_rstd(N, nchunks, sv, srcreg + "_" if False else srcreg, dim) if c0 == 0 else None
            if c0 != 0:
                b = nb()
                for c in range(nchunks):
                    sl = c % 2
                    S.add("act", lambda e, c=c, sl=sl: e.activation(out=sqr[:, sl, :N], in_=src[:, c0 + c, :N],
                                                                     func=AF.Square),
                          reads=["%s%d" % (srcreg, c0 + c)], writes=["sqr%d" % sl])
                    mm(psum[:, b, :N], onesb[:], sqr[:, sl, :N], c == 0, c == nchunks - 1, ["sqr%d" % sl, "onesb"], b)
                S.add("act", lambda e: e.activation(out=rs[:, :N], in_=psum[:, b, :N], func=AF.Sqrt, bias=epst[:],
                                                    scale=1.0 / dim), reads=["ps%d" % b, "epst"], writes=["rs"])
                S.add("dve", lambda e: e.reciprocal(out=rstd[:, :N], in_=rs[:, :N]), reads=["rs"], writes=["rstd"])
            for c in range(nchunks):
                S.add("dve", lambda e, c=c: e.scalar_tensor_tensor(out=dst[:, c0 + c, :N], in0=src[:, c0 + c, :N],
                                                                    scalar=pcol(gname, c), in1=rstd[:, :N],
                                                                    op0=ALU.mult, op1=ALU.mult),
                      reads=["%s%d" % (srcreg, c0 + c), "rstd", "prm"], writes=["%s%d" % (dstreg, c0 + c)])

        def fm_proj(N, units, src_tile, srcreg, nk=16):
            b = nb()
            for kc in range(nk):
                slot = units[kc // 8]
                mm(psum[:, b, :N], ring[:, slot, (kc % 8) * 128:(kc % 8 + 1) * 128], src_tile[:, kc, :N],
                   kc == 0, kc == nk - 1, ["wr%d" % slot, "%s%d" % (srcreg, kc)], b)
            return b

        def ffn(N, gname):
            norm(N, gname, KC, h, "h", hn, "hn", D)
            for g in range(NG):
                for j in range(G):
                    us = take(4)
                    bg = fm_proj(N, us[0:2], hn, "hn")
                    bu = fm_proj(N, us[2:4], hn, "hn")
                    par2 = j % 2
                    S.add("act", lambda e, bg=bg, par2=par2: e.activation(out=sg[:, par2, :N], in_=psum[:, bg, :N],
                                                                            func=AF.Silu),
                          reads=["ps%d" % bg], writes=["sg%d" % par2])
                    S.add("dve", lambda e, bu=bu, par2=par2, j=j: e.tensor_tensor(
                        out=abuf[:, j, :N], in0=sg[:, par2, :N], in1=psum[:, bu, :N], op=ALU.mult),
                        reads=["sg%d" % par2, "ps%d" % bu], writes=["a%d" % j])
                for half in range(2):
                    us = take(G)
                    for dcl in range(8):
                        dc = half * 8 + dcl
                        b = nb()
                        for j in range(G):
                            mm(psum[:, b, :N], ring[:, us[j], dcl * 128:(dcl + 1) * 128], abuf[:, j, :N],
                               j == 0, j == G - 1, ["wr%d" % us[j], "a%d" % j], b)
                        S.add("dve", lambda e, b=b, dc=dc: e.scalar_tensor_tensor(
                            out=h[:, dc, :N], in0=psum[:, b, :N], scalar=0.5, in1=h[:, dc, :N],
                            op0=ALU.mult, op1=ALU.add),
                            reads=["ps%d" % b, "h%d" % dc], writes=["h%d" % dc])

        def load_block(N, src_ap, t0):
            dma(lambda e: e.dma_start(out=h[:, :, :N],
                                      in_=src_ap.rearrange("(kc p) t -> p kc t", p=128)[:, :, t0:t0 + N]),
                writes=["h%d" % c for c in range(KC)], sem="ldh")

        def mixer_in(N, blk, sample):
            norm(N, "gm", KC, h, "h", hn, "hn", D)
            TS = min(128, N)
            nsub = N // TS
            t0 = 0 if sample else blk * NT
            for s in range(nsub):
                b = nb()
                lsl = s % 2
                for kc in range(KC):
                    mm(psum[:TS, b, 0:8], hn[:, kc, s * TS:(s + 1) * TS], wfb[:, kc, :], kc == 0, kc == KC - 1,
                       ["hn%d" % kc, "wfb"], b)
                S.add("dve", lambda e, b=b: e.tensor_tensor(out=lfe[:TS], in0=psum[:TS, b, 0:8], in1=pcol("bf", 0, 8)[:TS],
                                                            op=ALU.add), reads=["ps%d" % b, "prm"], writes=["lfe"])
                S.add("act", lambda e: e.activation(out=lfe[:TS], in_=lfe[:TS], func=AF.Exp, scale=-1.0),
                      reads=["lfe"], writes=["lfe"])
                S.add("act", lambda e: e.activation(out=lfe[:TS], in_=lfe[:TS], func=AF.Ln, bias=1.0),
                      reads=["lfe"], writes=["lfe"])
                S.add("dve", lambda e, lsl=lsl: e.tensor_scalar(out=lft[:TS, lsl, :], in0=lfe[:TS], scalar1=-1.0,
                                                                 scalar2=None, op0=ALU.mult),
                      reads=["lfe"], writes=["lft%d" % lsl])
                if sample:
                    dma(lambda e, lsl=lsl: e.dma_start(out=lfso[:, :], in_=lft[:TS, lsl, :]),
                        reads=["lft%d" % lsl], sem="olf%d" % lsl)
                    S.add("dve", lambda e, lsl=lsl: e.tensor_copy(out=lfS[:TS], in_=lft[:TS, lsl, :]),
                          reads=["lft%d" % lsl], writes=["lfS"])
                else:
                    sc = blk * 4 + s
                    dma(lambda e, sc=sc, lsl=lsl: e.dma_start(out=lfo[sc * 128:(sc + 1) * 128, :], in_=lft[:, lsl, :]),
                        reads=["lft%d" % lsl], sem="olf%d" % lsl)
                    b2 = nb()
                    mm(psum[:, b2, 0:8], pcol("tri", 0, 128), lft[:, lsl, :], True, False, ["lft%d" % lsl, "prm"], b2)
                    mm(psum[:, b2, 0:8], pcol("last", 0, 128), Ftm[:, sc, :], False, True, ["Ftm%d" % sc, "prm"], b2)
                    S.add("dve", lambda e, b2=b2, sc=sc: e.tensor_copy(out=Ftm[:, sc + 1, :], in_=psum[:, b2, 0:8]),
                          reads=["ps%d" % b2], writes=["Ftm%d" % (sc + 1)])
                    S.add("act", lambda e, b2=b2, sc=sc: e.activation(out=negF[:, sc, :], in_=psum[:, b2, 0:8],
                                                                       func=AF.Copy, scale=-1.0),
                          reads=["ps%d" % b2], writes=["negF%d" % sc])
            if not sample:
                b = nb()
                for s in range(4):
                    sc = blk * 4 + s
                    S.add("pe", lambda e, b=b, s=s, sc=sc: e.transpose(psum[0:8, b, s * 128:(s + 1) * 128],
                                                                        Ftm[:, sc + 1, :], pcol("ident", 0, 128)),
                          reads=["Ftm%d" % (sc + 1), "prm"], writes=["ps%d" % b])
                S.add("dve", lambda e, b=b: e.tensor_scalar(out=Fx[:], in0=psum[0:8, b, :], scalar1=1.0 / SCALE,
                                                            scalar2=None, op0=ALU.mult), reads=["ps%d" % b], writes=["Fx"])
                S.add("dve", lambda e: e.tensor_copy(out=Fhi[:, 0, :], in_=Fx[:]), reads=["Fx"], writes=["Fhi0"])
                S.add("dve", lambda e: e.tensor_tensor(out=Fr[:], in0=Fx[:], in1=Fhi[:, 0, :], op=ALU.subtract),
                      reads=["Fx", "Fhi0"], writes=["Fr"])
                S.add("dve", lambda e: e.tensor_copy(out=Fhi[:, 1, :], in_=Fr[:]), reads=["Fr"], writes=["Fhi1"])
                S.add("dve", lambda e: e.tensor_tensor(out=Fx[:], in0=Fr[:], in1=Fhi[:, 1, :], op=ALU.subtract),
                      reads=["Fr", "Fhi1"], writes=["Fx"])
                S.add("dve", lambda e: e.tensor_copy(out=Fhi[:, 2, :], in_=Fx[:]), reads=["Fx"], writes=["Fhi2"])
            if sample:
                ub, L = ubS, 4
            else:
                ub, L = ubP, NT
            for i in range(8):
                us = take(6)
                bcc = fm_proj(N, us[0:2], hn, "hn")
                S.add("act", lambda e, b=bcc: e.activation(out=ccs[:, :N], in_=psum[:, b, :N], func=AF.Copy),
                      reads=["ps%d" % bcc], writes=["ccs"])
                bch = fm_proj(N, us[2:4], hn, "hn")
                if sample:
                    S.add("dve", lambda e, i=i: e.tensor_copy(
                        out=ub[:, :, 0:2], in_=sth[:, i * 8:(i + 1) * 8].rearrange("p (b j) -> p b j", j=2)),
                        reads=["sth", "ub"], writes=["ub"])
                else:
                    S.add("dve", lambda e, i=i: e.tensor_copy(out=ub[:, 0, 0:2], in_=uhist[:, i, :]),
                          reads=["uhist", "ub"], writes=["ub"])
                S.add("dve", lambda e, b=bch: e.tensor_tensor(
                    out=ub[:, :, 2:L + 2], in0=ccs[:, :N].rearrange("p (b l) -> p b l", l=L),
                    in1=psum[:, b, :N].rearrange("p (b l) -> p b l", l=L), op=ALU.mult),
                    reads=["ccs", "ps%d" % bch, "ub"], writes=["ub"])
                bcb = fm_proj(N, us[4:6], hn, "hn")
                S.add("act", lambda e, b=bcb: e.activation(out=cbs[:, :N], in_=psum[:, b, :N], func=AF.Copy),
                      reads=["ps%d" % bcb], writes=["cbs"])
                y13 = y1[:, :N].rearrange("p (b l) -> p b l", l=L)
                S.add("dve", lambda e, i=i, y13=y13: e.tensor_scalar(out=y13, in0=ub[:, :, 0:L], scalar1=pcol("cw", i),
                                                                     scalar2=None, op0=ALU.mult),
                      reads=["ub", "prm"], writes=["y1"])
                S.add("dve", lambda e, i=i, y13=y13: e.scalar_tensor_tensor(
                    out=y13, in0=ub[:, :, 1:L + 1], scalar=pcol("cw", 8 + i), in1=y13, op0=ALU.mult, op1=ALU.add),
                    reads=["ub", "y1", "prm"], writes=["y1"])
                S.add("dve", lambda e, i=i, y13=y13: e.scalar_tensor_tensor(
                    out=y13, in0=ub[:, :, 2:L + 2], scalar=pcol("cw", 16 + i), in1=y13, op0=ALU.mult, op1=ALU.add),
                    reads=["ub", "y1", "prm"], writes=["y1"])
                S.add("dve", lambda e, i=i: e.tensor_tensor(out=zb[:, i, :N], in0=y1[:, :N], in1=cbs[:, :N], op=ALU.mult),
                      reads=["y1", "cbs"], writes=["zb%d" % i])
                if sample:
                    S.add("dve", lambda e, i=i: e.tensor_copy(
                        out=cvs[:, i * 8:(i + 1) * 8].rearrange("p (b j) -> p b j", j=2), in_=ub[:, :, L:L + 2]),
                        reads=["ub"], writes=["cvs"])
                else:
                    S.add("dve", lambda e, i=i: e.tensor_copy(out=uhist[:, i, :], in_=ub[:, 0, L:L + 2]),
                          reads=["ub"], writes=["uhist"])
            if sample:
                dma(lambda e: e.dma_start(out=cvso[:], in_=cvs[:]), reads=["cvs"], sem="ocvs")
            elif blk == NBLK - 1:
                dma(lambda e: e.dma_start(out=cvo[:], in_=uhist[:].rearrange("p a b -> p (a b)")),
                    reads=["uhist"], sem="ocv")
            norm(N, "gc", 8, zb, "zb", zb, "zb", CONV)
            for hh in range(NH):
                us = take(2)
                bk = fm_proj(N, us, hn, "hn")
                ksl = hh % 2
                S.add("act", lambda e, b=bk, ksl=ksl: e.activation(out=kst[:, ksl, :N], in_=psum[:, b, :N], func=AF.Copy),
                      reads=["ps%d" % bk], writes=["kst%d" % ksl])
                dst = ksT if sample else kT
                dma(lambda e, hh=hh, ksl=ksl, dst=dst: e.dma_start(
                    out=dst[hh * 128:(hh + 1) * 128, t0:t0 + N], in_=kst[:, ksl, :N]),
                    reads=["kst%d" % ksl], writes=[] if sample else ["kTd_%d_%d" % (hh, blk)], sem="ok%d" % ksl)
                if sample:
                    S.add("dve", lambda e, hh=hh, ksl=ksl: e.tensor_copy(out=knT[:, hh, :], in_=kst[:, ksl, :N]),
                          reads=["kst%d" % ksl], writes=["knT"])
            for cb in range(2):
                banks = [nb() for _ in range(nsub)]
                for u in range(8):
                    us = take(1)
                    for s in range(nsub):
                        for kl in range(2):
                            kc = 2 * u + kl
                            mm(psum[:TS, banks[s], :], hn[:, kc, s * TS:(s + 1) * TS],
                               ring[:, us[0], kl * 512:(kl + 1) * 512], kc == 0, kc == KC - 1,
                               ["hn%d" % kc, "wr%d" % us[0]], banks[s])
                for s in range(nsub):
                    vsl = s % 2
                    S.add("act", lambda e, b=banks[s], vsl=vsl: e.activation(
                        out=vst[:TS, vsl, :], in_=psum[:TS, b, :], func=AF.Copy),
                        reads=["ps%d" % banks[s]], writes=["vst%d" % vsl])
                    dst = vso if sample else vo
                    r0 = t0 + s * TS
                    dma(lambda e, vsl=vsl, cb=cb, dst=dst, r0=r0: e.dma_start(
                        out=dst[r0:r0 + TS, cb * 512:(cb + 1) * 512], in_=vst[:TS, vsl, :]),
                        reads=["vst%d" % vsl], writes=[] if sample else ["vod_%d_%d" % (blk * 4 + s, cb)],
                        sem="ov%d" % vsl)
                    if sample:
                        S.add("dve", lambda e, vsl=vsl, cb=cb: e.tensor_copy(out=vnw[:, cb * 512:(cb + 1) * 512],
                                                                             in_=vst[:TS, vsl, :]),
                              reads=["vst%d" % vsl], writes=["vnw"])

        kvctr = [0]
        pctr = [0]

        def attention_prompt(blk):
            N = NT
            nsc = 4 * blk + 4
            for hh in range(NH):
                us = take(2)
                bq = fm_proj(N, us, hn, "hn")
                qsl = hh % 2
                S.add("act", lambda e, b=bq, qsl=qsl: e.activation(out=qT[:, qsl, :], in_=psum[:, b, :], func=AF.Copy),
                      reads=["ps%d" % bq], writes=["qT%d" % qsl])
                bo, bd = accpair()
                for sc in range(nsc):
                    diag = sc >= 4 * blk
                    cs = (sc - 4 * blk) * 128 if diag else 0
                    slot = kvctr[0] % 4
                    kvctr[0] += 1
                    dma(lambda e, slot=slot, hh=hh, sc=sc: e.dma_start(
                        out=kvk[:, slot, :], in_=kT[hh * 128:(hh + 1) * 128, sc * 128:(sc + 1) * 128]),
                        reads=["kTd_%d_%d" % (hh, sc // 4)], writes=["kvk%d" % slot], sem="kvk%d" % slot)
                    dma(lambda e, slot=slot, hh=hh, sc=sc: e.dma_start(
                        out=kvv[:, slot, :], in_=vo[sc * 128:(sc + 1) * 128, hh * 128:(hh + 1) * 128]),
                        reads=["vod_%d_%d" % (sc, hh // 4)], writes=["kvv%d" % slot], sem="kvv%d" % slot)
                    bs = nb()
                    mm(psum[:, bs, cs:N], kvk[:, slot, :], qT[:, qsl, cs:N], True, False,
                       ["kvk%d" % slot, "qT%d" % qsl], bs)
                    for part in range(3):
                        mm(psum[:, bs, cs:N], selb[:, hh * 128:(hh + 1) * 128], Fhi[:, part, cs:N], False,
                           (part == 2 and not diag), ["selb", "Fhi%d" % part], bs)
                    if diag:
                        mm(psum[:, bs, cs:cs + 128], identb[:], maskb[:], False, True, ["identb", "maskb"], bs)
                    psl = pctr[0] % 3
                    pctr[0] += 1
                    S.add("act", lambda e, bs=bs, psl=psl, cs=cs, sc=sc, hh=hh: e.activation(
                        out=pTt[:, psl, cs:N], in_=psum[:, bs, cs:N], func=AF.Exp, bias=negF[:, sc, hh:hh + 1],
                        scale=SCALE), reads=["ps%d" % bs, "negF%d" % sc], writes=["pT%d" % psl])
                    mm(psum[:, bo, cs:N], kvv[:, slot, :], pTt[:, psl, cs:N], sc == 0, sc == nsc - 1,
                       ["kvv%d" % slot, "pT%d" % psl], bo)
                    mm(psum[:, bd, cs:N], onesf[:], pTt[:, psl, cs:N], sc == 0, sc == nsc - 1,
                       ["onesf", "pT%d" % psl], bd)
                S.add("dve", lambda e, bd=bd: e.reciprocal(out=rden[:], in_=psum[:, bd, :]),
                      reads=["ps%d" % bd], writes=["rden"])
                S.add("dve", lambda e, bo=bo, hh=hh: e.tensor_tensor(out=zb[:, 8 + hh, :], in0=psum[:, bo, :],
                                                                     in1=rden[:], op=ALU.mult),
                      reads=["ps%d" % bo, "rden"], writes=["zb%d" % (8 + hh)])

        def mixer_out(N):
            norm(N, "ga", 8, zb, "zb", zb, "zb", CONV, c0=8)
            for dc in range(KC):
                us = take(2)
                b = fm_proj(N, us, zb, "zb")
                S.add("dve", lambda e, b=b, dc=dc: e.tensor_tensor(out=h[:, dc, :N], in0=psum[:, b, :N],
                                                                   in1=h[:, dc, :N], op=ALU.add),
                      reads=["ps%d" % b, "h%d" % dc], writes=["h%d" % dc])

        def ple(N, p_ap, t0):
            norm(N, "gp", KC, h, "h", hn, "hn", D)
            dma(lambda e: e.dma_start(out=kst[:, :, :N], in_=p_ap.rearrange("(kc p) t -> p kc t", p=128)[:, :, t0:t0 + N]),
                writes=["kst0", "kst1"], sem="ldp")
            S.add("dve", lambda e: e.tensor_copy(out=ptb[:, :, :N], in_=kst[:, :, :N]), reads=["kst0", "kst1"],
                  writes=["ptb"])
            for q in range(4):
                pu = take(1)[0]
                pbanks = []
                for dcl in range(4):
                    b = 4 + dcl
                    pbanks.append(b)
                    for kc in range(2):
                        mm(psum[:, b, :N], ring[:, pu, (dcl * 2 + kc) * 128:(dcl * 2 + kc + 1) * 128], ptb[:, kc, :N],
                           kc == 0, kc == 1, ["wr%d" % pu, "ptb"], b)
                for dcl in range(4):
                    dc = 4 * q + dcl
                    us = take(2)
                    bgt = fm_proj(N, us, hn, "hn")
                    S.add("act", lambda e, b=bgt: e.activation(out=ccs[:, :N], in_=psum[:, b, :N], func=AF.Sigmoid),
                          reads=["ps%d" % bgt], writes=["ccs"])
                    pb = pbanks[dcl]
                    S.add("dve", lambda e, pb=pb: e.tensor_tensor(out=y1[:, :N], in0=ccs[:, :N], in1=psum[:, pb, :N],
                                                                  op=ALU.mult), reads=["ccs", "ps%d" % pb], writes=["y1"])
                    S.add("dve", lambda e, dc=dc: e.tensor_tensor(out=h[:, dc, :N], in0=h[:, dc, :N], in1=y1[:, :N],
                                                                  op=ALU.add), reads=["y1", "h%d" % dc], writes=["h%d" % dc])

        def final_out(N, dst, t0):
            sumsq_rstd(N, KC, h, "h", D)
            for c in range(KC):
                sl = c % 2
                S.add("dve", lambda e, c=c, sl=sl: e.scalar_tensor_tensor(
                    out=vst[:, sl, :N], in0=h[:, c, :N], scalar=pcol("gf", c), in1=rstd[:, :N],
                    op0=ALU.mult, op1=ALU.mult), reads=["h%d" % c, "rstd", "prm"], writes=["vst%d" % sl])
                dma(lambda e, c=c, sl=sl: e.dma_start(out=dst[c * 128:(c + 1) * 128, t0:t0 + N], in_=vst[:, sl, :N]),
                    reads=["vst%d" % sl], sem="ov%d" % sl)

        def attention_sample():
            N = NS
            for hh in range(NH):
                us = take(2)
                b = fm_proj(N, us, hn, "hn")
                S.add("act", lambda e, b=b, hh=hh: e.activation(out=qTs[:, hh, :], in_=psum[:, b, :N], func=AF.Copy),
                      reads=["ps%d" % b], writes=["qTs"])
            b = nb()
            btri16 = prm[0:NS, PC["btri"][0]:PC["btri"][0] + NS]
            mm(psum[:NS, b, 0:8], btri16, lfS[:NS, :], True, True, ["lfS", "prm"], b)
            S.add("act", lambda e, b=b: e.activation(out=negE[:], in_=psum[:NS, b, 0:8], func=AF.Copy, scale=-1.0),
                  reads=["ps%d" % b], writes=["negE"])
            bo, bd = accpair()
            cctr = 0
            for bi in range(4):
                ib = bi % 2
                dma(lambda e, bi=bi, ib=ib: e.dma_start(out=ptbc[:, ib, :], in_=ptab[bi:bi + 1, :].partition_broadcast(128)),
                    writes=["ptbc%d" % ib], sem="ptb%d" % ib)
                dma(lambda e, bi=bi, ib=ib: e.dma_start(out=pcolb[:, ib:ib + 1],
                                                        in_=ptab[bi:bi + 1, :].rearrange("a (p f) -> p (a f)", f=1)),
                    writes=["pcolb%d" % ib], sem="ptc%d" % ib)
                S.add("dve", lambda e, ib=ib: e.tensor_scalar(out=idx0f[:], in0=ptbc[:, ib, :], scalar1=128.0,
                                                               scalar2=pcol("iota", 0, 1), op0=ALU.mult, op1=ALU.add),
                      reads=["ptbc%d" % ib, "prm"], writes=["idx0f"])
                for hh in range(NH):
                    S.add("dve", lambda e, hh=hh: e.tensor_scalar(out=idxh[:, hh, :], in0=idx0f[:],
                                                                   scalar1=float(hh * NPOOL * 128), scalar2=None,
                                                                   op0=ALU.add),
                          reads=["idx0f"], writes=["idxh%d" % hh])
                    S.add("dve", lambda e, hh=hh, ib=ib: e.tensor_scalar(out=pcolh[:, hh:hh + 1], in0=pcolb[:, ib:ib + 1],
                                                                          scalar1=float(hh * NPOOL), scalar2=None,
                                                                          op0=ALU.add),
                          reads=["pcolb%d" % ib], writes=["pcolh%d" % hh])
                for hh in range(NH):
                    col0 = (bi * NH + hh) * 4
                    dcol0 = (bi * NH + hh) * 4 * PCH
                    dma(lambda e, hh=hh: e.indirect_dma_start(
                        out=Lb[:, :], out_offset=None, in_=clf[:, :],
                        in_offset=bass.IndirectOffsetOnAxis(ap=pcolh[:, hh:hh + 1], axis=0)),
                        reads=["pcolh%d" % hh], writes=["Lb"], sem="lb", eng="pool")
                    b = nb()
                    S.add("pe", lambda e, b=b: e.transpose(psum[:, b, 0:64], Lb[:, :],
                                                            prm[0:64, PC["ident"][0]:PC["ident"][0] + 64]),
                          reads=["Lb", "prm"], writes=["ps%d" % b])
                    S.add("dve", lambda e, b=b: e.tensor_copy(out=LT[:], in_=psum[:, b, 0:64]),
                          reads=["ps%d" % b], writes=["LT"])
                    ba, bt = nb(), nb()
                    mm(psum[:, ba, 0:64], pcol("ustr", 0, 128), LT[:], True, True, ["LT", "prm"], ba)
                    mm(psum[:, bt, 0:64], onesf[:], LT[:], True, True, ["LT", "onesf"], bt)
                    S.add("dve", lambda e, bt=bt: e.tensor_copy(out=Dm[:], in_=psum[:, bt, 0:64]),
                          reads=["ps%d" % bt], writes=["Dm"])
                    S.add("dve", lambda e: e.tensor_tensor_scan(out=Csc[:], data0=ones64[:], data1=Dm[:], initial=0.0,
                                                                op0=ALU.mult, op1=ALU.add),
                          reads=["Dm", "ones64"], writes=["Csc"])
                    S.add("dve", lambda e, ba=ba: e.scalar_tensor_tensor(out=Dm[:], in0=Csc[:], scalar=-1.0,
                                                                         in1=psum[:, ba, 0:64], op0=ALU.mult, op1=ALU.add),
                          reads=["Csc", "ps%d" % ba, "Dm"], writes=["Dm"])
                    S.add("dve", lambda e: e.tensor_scalar(out=Dm[:], in0=Dm[:], scalar1=Csc[:, 63:64], scalar2=None,
                                                           op0=ALU.add), reads=["Dm", "Csc"], writes=["Dm"])
                    for ch in range(NPG // PCH):
                        sl = cctr % 2
                        cctr += 1
                        for pp in range(PCH):
                            j = ch * PCH + pp
                            dma(lambda e, hh=hh, j=j, sl=sl, pp=pp: e.indirect_dma_start(
                                out=kvp[:, sl, pp, :], out_offset=None, in_=ckv[:, :],
                                in_offset=bass.IndirectOffsetOnAxis(ap=idxh[:, hh, j:j + 1], axis=0)),
                                reads=["idxh%d" % hh], writes=["kvp%d_%d" % (sl, pp)], sem="pg%d_%d" % (sl, pp), eng="pool")
                        bs = nb()
                        for pp in range(PCH):
                            mm(psum[:, bs, pp * 4:(pp + 1) * 4], kvp[:, sl, pp, 0:128],
                               qTs[:, hh, bi * 4:(bi + 1) * 4], True, True, ["kvp%d_%d" % (sl, pp), "qTs"], bs)
                        S.add("dve", lambda e, bs=bs, ch=ch: e.scalar_tensor_tensor(
                            out=tmpS[:].rearrange("p (a t) -> p a t", t=4),
                            in0=psum[:, bs, 0:PCH * 4].rearrange("p (a t) -> p a t", t=4), scalar=SCALE,
                            in1=Dm[:, ch * PCH:(ch + 1) * PCH].unsqueeze(2).to_broadcast([128, PCH, 4]),
                            op0=ALU.mult, op1=ALU.add), reads=["ps%d" % bs, "Dm"], writes=["tmpS"])
                        S.add("act", lambda e, sl=sl: e.activation(out=Pm[:, sl, :], in_=tmpS[:], func=AF.Exp),
                              reads=["tmpS"], writes=["Pm%d" % sl])
                        mm(psum[:, bd, dcol0:dcol0 + 4 * PCH], onesf[:], Pm[:, sl, :], ch == 0, False,
                           ["onesf", "Pm%d" % sl], bd)
                        for pp in range(PCH):
                            first = (ch == 0 and pp == 0)
                            mm(psum[:, bo, col0:col0 + 4], kvp[:, sl, pp, 128:256],
                               Pm[:, sl, pp * 4:(pp + 1) * 4], first, False, ["kvp%d_%d" % (sl, pp), "Pm%d" % sl], bo)
                    bs = nb()
                    mm(psum[:NS, bs, 0:4], knT[:, hh, :], qTs[:, hh, bi * 4:(bi + 1) * 4], True, True, ["knT", "qTs"], bs)
                    S.add("dve", lambda e, bs=bs, bi=bi: e.scalar_tensor_tensor(
                        out=tmpS[:NS, 0:4], in0=psum[:NS, bs, 0:4], scalar=SCALE, in1=pcol("maskS", bi * 4, 4)[:NS],
                        op0=ALU.mult, op1=ALU.add), reads=["ps%d" % bs, "prm"], writes=["tmpS"])
                    sl = cctr % 2
                    cctr += 1
                    S.add("act", lambda e, sl=sl, hh=hh: e.activation(out=Pm[:NS, sl, 0:4], in_=tmpS[:NS, 0:4], func=AF.Exp,
                                                                       bias=negE[:, hh:hh + 1]),
                          reads=["tmpS", "negE"], writes=["Pm%d" % sl])
                    mm(psum[:, bo, col0:col0 + 4], vnw[:, hh * 128:(hh + 1) * 128], Pm[:NS, sl, 0:4], False, True,
                       ["vnw", "Pm%d" % sl], bo)
                    mm(psum[:, bd, dcol0:dcol0 + 4], onesf[:NS, :], Pm[:NS, sl, 0:4], False, True, ["onesf", "Pm%d" % sl], bd)
            S.add("dve", lambda e: e.tensor_reduce(
                out=rs[:, :128].rearrange("p (g t) -> p g t", t=4),
                in_=psum[:, bd, :].rearrange("p (g a t) -> p g t a", a=PCH, t=4),
                axis=mybir.AxisListType.X, op=ALU.add), reads=["ps%d" % bd], writes=["rs"])
            S.add("dve", lambda e: e.reciprocal(out=rden[:, :128], in_=rs[:, :128]), reads=["rs"], writes=["rden"])
            S.add("dve", lambda e: e.tensor_tensor(out=osb[:], in0=psum[:, bo, :128], in1=rden[:, :128], op=ALU.mult),
                  reads=["ps%d" % bo, "rden"], writes=["osb"])
            for hh in range(NH):
                S.add("dve", lambda e, hh=hh: e.tensor_copy(
                    out=zb[:, 8 + hh, 0:NS].rearrange("p (i t) -> p i t", t=4),
                    in_=osb[:].rearrange("p (i h t) -> p i h t", h=NH, t=4)[:, :, hh, :]),
                    reads=["osb"], writes=["zb%d" % (8 + hh)])

        for blk in range(NBLK):
            load_block(NT, xT, blk * NT)
            ffn(NT, "g1")
            mixer_in(NT, blk, False)
            attention_prompt(blk)
            mixer_out(NT)
            ffn(NT, "g2")
            ple(NT, pT, blk * NT)
            final_out(NT, yT, blk * NT)
        if WITH_SAMPLE:
            load_block(NS, xsT, 0)
            ffn(NS, "g1")
            mixer_in(NS, None, True)
            attention_sample()
            mixer_out(NS)
            ffn(NS, "g2")
            ple(NS, psT, 0)
            final_out(NS, ysT, 0)
            assert wstate["next_use"] == len(plan), (wstate["next_use"], len(plan))

        finals = [n for n in dsem if n[0] == "o"]
        S.emit(nc, None, esem, dsem, finals)
    return nc


_CACHE = {}


def kernel(**inp):
    inp = {k: np.asarray(v) for k, v in inp.items()}
    if "nc" not in _CACHE:
        _CACHE["nc"] = build_program()
    nc = _CACHE["nc"]
    wf = np.ascontiguousarray(inp["w_in"][0][:, 6144:6152].reshape(16, 128, 8).transpose(1, 0, 2).reshape(128, 128))
    base = pack_weights(inp, 0)
    if WITH_SAMPLE:
        ck = inp["cache_k"][0].transpose(2, 0, 3, 1)
        cv = inp["cache_v"][0].transpose(2, 0, 1, 3)
        ckv = np.concatenate([ck, cv], axis=3).reshape(NH * NPOOL * 128, 256)
        clf = np.ascontiguousarray(inp["cache_logf"][0].transpose(2, 0, 1)).reshape(NH * NPOOL, 128)
    in_maps = []
    for c in range(8):
        m = {
            "xT": np.ascontiguousarray(inp["x_prompt"][c].T),
            "pT": np.ascontiguousarray(inp["p_prompt"][0, c].T),
            "xsT": np.ascontiguousarray(inp["x_sample"][4 * c:4 * c + 4].reshape(NS, D).T),
            "psT": np.ascontiguousarray(inp["p_sample"][0, 4 * c:4 * c + 4].reshape(NS, PLE).T),
            "stc": np.ascontiguousarray(inp["state_conv"][0, 4 * c:4 * c + 4].reshape(4, 2, 8, 128)
                                        .transpose(3, 2, 0, 1)).reshape(128, 64),
            "wst": base, "par": pack_params(inp, c), "wfd": wf,
        }
        if WITH_SAMPLE:
            m["ckv"] = ckv
            m["clf"] = clf
            m["ptab"] = np.ascontiguousarray(inp["page_table"][4 * c:4 * c + 4].astype(np.int32))
        in_maps.append(m)
    res = run_bass_kernel_spmd(nc, in_maps, core_ids=list(range(8))).results
    y_p = np.stack([res[c]["yT"].T for c in range(8)], 0)
    k_p = np.stack([res[c]["kT"].T.reshape(SEQ, NH, HD) for c in range(8)], 0)[None]
    v_p = np.stack([res[c]["vo"].reshape(SEQ, NH, HD) for c in range(8)], 0)[None]
    lf_p = np.stack([res[c]["lfo"] for c in range(8)], 0)[None]
    cv_p = np.stack([res[c]["cvo"].reshape(128, 8, 2).transpose(2, 1, 0).reshape(2, CONV) for c in range(8)], 0)[None]

    def cat(name, fn):
        return np.concatenate([fn(res[c][name]) for c in range(8)], 0)
    y_s = cat("ysT", lambda a: a.T.reshape(4, 4, D))
    k_s = cat("ksT", lambda a: a.T.reshape(4, 4, NH, HD))[None]
    v_s = cat("vso", lambda a: a.reshape(4, 4, NH, HD))[None]
    lf_s = cat("lfso", lambda a: a.reshape(4, 4, NH))[None]
    cv_s = cat("cvso", lambda a: a.reshape(128, 8, 4, 2).transpose(2, 3, 1, 0).reshape(4, 2, CONV))[None]
    f = np.float32
    return tuple(np.ascontiguousarray(a, f) for a in (y_p, y_s, k_p, v_p, lf_p, cv_p, k_s, v_s, lf_s, cv_s))
```

```python
import contextlib
import numpy as np
import concourse.bass as bass
import concourse.mybir as mybir
from concourse.bass_utils import run_bass_kernel_spmd

F32 = mybir.dt.float32
BF16 = mybir.dt.bfloat16
AF = mybir.ActivationFunctionType
ALU = mybir.AluOpType

D = 2048
KC = 16
DFF = 5632
NFC = 44
G = 4
NG = NFC // G
CONV = 1024
NH = 8
HD = 128
PLE = 256
SEQ = 2048
NBLK = 4
NT = 512
WITH_SAMPLE = True
NS = 16
EPS = 1e-6
SCALE = HD ** -0.5
RING = 12
NSTG = 6
LOOK = 10


def _fm_units(W, col0):
    blk = W[:, col0:col0 + 128].reshape(16, 128, 128)
    return [np.ascontiguousarray(blk[half * 8:(half + 1) * 8].transpose(1, 0, 2)).reshape(128, 1024)
            for half in range(2)]


def pack_weights(inp, core):
    units = []

    def ffn(wg, wu, wd):
        for g in range(NG):
            for j in range(G):
                fc = g * G + j
                units.extend(_fm_units(wg, fc * 128))
                units.extend(_fm_units(wu, fc * 128))
            for half in range(2):
                for j in range(G):
                    fc = g * G + j
                    units.append(np.ascontiguousarray(wd[fc * 128:(fc + 1) * 128, half * 1024:(half + 1) * 1024]))

    ffn(inp["w_ffn1_gate"][0], inp["w_ffn1_up"][0], inp["w_ffn1_down"][0])
    win = inp["w_in"][0]
    for i in range(8):
        units.extend(_fm_units(win, 1024 + 128 * i))
        units.extend(_fm_units(win, 2048 + 128 * i))
        units.extend(_fm_units(win, 128 * i))
    for h in range(NH):
        units.extend(_fm_units(win, 4096 + 128 * h))
    for cb in range(2):
        col0 = 5120 + 512 * cb
        blk = win[:, col0:col0 + 512].reshape(16, 128, 512)
        for u in range(8):
            units.append(np.ascontiguousarray(blk[2 * u:2 * u + 2].transpose(1, 0, 2)).reshape(128, 1024))
    for h in range(NH):
        units.extend(_fm_units(win, 3072 + 128 * h))
    wo = inp["w_out"][0]
    for dc in range(16):
        units.extend(_fm_units(wo, dc * 128))
    ffn(inp["w_ffn2_gate"][0], inp["w_ffn2_up"][0], inp["w_ffn2_down"][0])
    wpg = inp["w_ple_gate"][0]
    wpp = inp["w_ple_proj"][0]
    for q in range(4):
        blk = wpp[:, q * 512:(q + 1) * 512].reshape(2, 128, 4, 128)
        units.append(np.ascontiguousarray(blk.transpose(1, 2, 0, 3)).reshape(128, 1024))
        for dcl in range(4):
            units.extend(_fm_units(wpg, (4 * q + dcl) * 128))
    assert len(units) == NU, len(units)
    return np.stack(units, axis=0)


NU_FFN = NG * (G * 4 + G * 2)
NU_IN = 8 * 6 + NH * 2 + 16
NU = 2 * NU_FFN + NU_IN + NH * 2 + 32 + 4 + 32
NX = 6


def pack_extras(inp, core):
    win = inp["w_in"][0]
    return np.stack(_fm_units(win, 3072 + 128 * core) + _fm_units(win, 4096 + 128 * core) +
                    _fm_units(win, 5120 + 128 * core), axis=0)


PC = {}
_o = 0
for _n, _w in [("g1", 16), ("gm", 16), ("g2", 16), ("gp", 16), ("gf", 16), ("gc", 8), ("ga", 8), ("cw", 24),
               ("bf", 8), ("hsel", 8), ("iota", 1), ("ident", 128), ("tri", 128), ("last", 128), ("mask", 128), ("ustr", 128),
               ("maskS", 128), ("btri", 128), ("sel", 1024)]:
    PC[_n] = (_o, _w)
    _o += _w
NPAR = _o


def pack_params(inp, core):
    P = np.zeros((128, NPAR), np.float32)

    def put(name, arr):
        o, w = PC[name]
        P[:, o:o + w] = arr

    put("g1", inp["norm_ffn1"][0].reshape(16, 128).T)
    put("gm", inp["norm_mix"][0].reshape(16, 128).T)
    put("g2", inp["norm_ffn2"][0].reshape(16, 128).T)
    put("gp", inp["norm_ple"][0].reshape(16, 128).T)
    put("gf", inp["norm_final"].reshape(16, 128).T)
    put("gc", inp["norm_conv_out"][0].reshape(8, 128).T)
    put("ga", inp["norm_attn_out"][0].reshape(8, 128).T)
    put("cw", inp["conv_w"][0].reshape(3, 8, 128).transpose(2, 0, 1).reshape(128, 24))
    put("bf", np.broadcast_to(inp["b_f"][0][None, :], (128, 8)))
    hs = np.zeros((128, 8), np.float32)
    hs[:, core] = 1.0
    put("hsel", hs)
    put("iota", np.arange(128, dtype=np.float32)[:, None])
    put("ident", np.eye(128, dtype=np.float32))
    ar = np.arange(128)
    put("tri", (ar[:, None] <= ar[None, :]).astype(np.float32))
    last = np.zeros((128, 128), np.float32)
    last[127, :] = 1.0
    put("last", last)
    put("mask", np.where(ar[:, None] <= ar[None, :], 0.0, -30000.0).astype(np.float32))
    put("ustr", (ar[:, None] > ar[None, :]).astype(np.float32))
    same = (ar[:, None] // 4) == (ar[None, :] // 4)
    put("maskS", np.where(same & (ar[:, None] <= ar[None, :]), 0.0, -30000.0).astype(np.float32))
    put("btri", (same & (ar[:, None] <= ar[None, :])).astype(np.float32))
    sel = np.zeros((128, 8, 128), np.float32)
    for h in range(8):
        sel[h, h, :] = 1.0
    put("sel", sel.reshape(128, 1024))
    return P


class Sched:
    def __init__(self):
        self.ops = {e: [] for e in ("pe", "act", "dve", "pool", "sp")}
        self.regions = {}
        self.dma_cnt = {}

    def add(self, eng, fn, reads=(), writes=(), dma=None):
        deps = []
        for r in reads:
            reg = self.regions.get(r)
            if reg is not None and reg["w"] is not None:
                deps.append(reg["w"])
        for w in writes:
            reg = self.regions.get(w)
            if reg is not None:
                if reg["w"] is not None:
                    deps.append(reg["w"])
                deps.extend(reg["r"].values())
        idx = len(self.ops[eng])
        if dma is not None:
            k = self.dma_cnt.get(dma, 0) + 1
            self.dma_cnt[dma] = k
            ref = ("dma", dma, k)
            key = "dma:" + dma
        else:
            ref = ("eng", eng, idx)
            key = eng
        if eng == "pe":
            deps = [d for d in deps if not (d[0] == "eng" and d[1] == "pe")]
        self.ops[eng].append({"fn": fn, "deps": deps, "marked": False, "dma": dma})
        for r in reads:
            reg = self.regions.setdefault(r, {"w": None, "r": {}})
            reg["r"][key] = ref
        for w in writes:
            self.regions[w] = {"w": ref, "r": {}}
        return ref

    def emit(self, nc, engines, esem, dsem, final_waits):
        for e, lst in self.ops.items():
            for op in lst:
                for d in op["deps"]:
                    if d[0] == "eng":
                        self.ops[d[1]][d[2]]["marked"] = True
        cum = {}
        for e, lst in self.ops.items():
            c = 0
            arr = []
            for op in lst:
                if op["marked"]:
                    c += 1
                arr.append(c)
            cum[e] = arr

        def run(e, eng):
            waited = {}
            for i, op in enumerate(self.ops[e]):
                need = {}
                for d in op["deps"]:
                    if d[0] == "eng":
                        if d[1] == e and d[2] >= i:
                            continue
                        s, v = ("e", d[1]), cum[d[1]][d[2]]
                    else:
                        s, v = ("d", d[1]), 16 * d[2]
                    if v > need.get(s, 0):
                        need[s] = v
                for s, v in need.items():
                    if waited.get(s, 0) >= v:
                        continue
                    waited[s] = v
                    sem = esem[s[1]] if s[0] == "e" else dsem[s[1]]
                    eng.wait_ge(sem, v)
                ins = op["fn"](eng)
                if op["dma"] is not None:
                    ins.then_inc(dsem[op["dma"]], 16)
                elif op["marked"]:
                    ins.then_inc(esem[e], 1)
            if e == "sp":
                for name in final_waits:
                    eng.wait_ge(dsem[name], 16 * self.dma_cnt[name])

        with nc.Block() as block:
            @block.sync
            def _(eng):
                run("sp", eng)

            @block.tensor
            def _(eng):
                run("pe", eng)

            @block.scalar
            def _(eng):
                run("act", eng)

            @block.vector
            def _(eng):
                run("dve", eng)

            @block.gpsimd
            def _(eng):
                run("pool", eng)


NPOOL = 2560
NPG = 64
PCH = 4


def build_program():
    nc = bass.Bass("TRN2", target_bir_lowering=False)
    S = Sched()
    dt = nc.dram_tensor
    I32 = mybir.dt.int32
    xT = dt("xT", [D, SEQ], F32, kind="ExternalInput").ap()
    pT = dt("pT", [PLE, SEQ], F32, kind="ExternalInput").ap()
    xsT = dt("xsT", [D, NS], F32, kind="ExternalInput").ap()
    psT = dt("psT", [PLE, NS], F32, kind="ExternalInput").ap()
    stc = dt("stc", [128, 64], F32, kind="ExternalInput").ap()
    wst = dt("wst", [NU, 128, 1024], F32, kind="ExternalInput").ap()
    par = dt("par", [128, NPAR], F32, kind="ExternalInput").ap()
    wfd = dt("wfd", [128, KC * 8], F32, kind="ExternalInput").ap()
    if WITH_SAMPLE:
        ckv = dt("ckv", [NH * NPOOL * 128, 256], F32, kind="ExternalInput").ap()
        clf = dt("clf", [NH * NPOOL, 128], F32, kind="ExternalInput").ap()
        ptab = dt("ptab", [4, NPG], I32, kind="ExternalInput").ap()
    yT = dt("yT", [D, SEQ], F32, kind="ExternalOutput").ap()
    kT = dt("kT", [CONV, SEQ], F32, kind="ExternalOutput").ap()
    vo = dt("vo", [SEQ, CONV], F32, kind="ExternalOutput").ap()
    lfo = dt("lfo", [SEQ, NH], F32, kind="ExternalOutput").ap()
    cvo = dt("cvo", [128, 16], F32, kind="ExternalOutput").ap()
    ysT = dt("ysT", [D, NS], F32, kind="ExternalOutput").ap()
    ksT = dt("ksT", [CONV, NS], F32, kind="ExternalOutput").ap()
    vso = dt("vso", [NS, CONV], F32, kind="ExternalOutput").ap()
    lfso = dt("lfso", [NS, NH], F32, kind="ExternalOutput").ap()
    cvso = dt("cvso", [128, 64], F32, kind="ExternalOutput").ap()

    es = contextlib.ExitStack()
    with es:
        def sb(name, shape, dtype):
            return es.enter_context(nc.sbuf_tensor(name, shape, dtype))

        prm = sb("prm", [128, NPAR], F32)
        onesb = sb("onesb", [128, 128], BF16)
        onesf = sb("onesf", [128, 128], F32)
        identb = sb("identb", [128, 128], BF16)
        maskb = sb("maskb", [128, 128], BF16)
        selb = sb("selb", [8, 1024], BF16)
        epst = sb("epst", [128, 1], F32)
        h = sb("h", [128, KC, NT], F32)
        hn = sb("hn", [128, KC, NT], BF16)
        sqr = sb("sqr", [128, 2, NT], BF16)
        rs = sb("rs", [128, NT], F32)
        rstd = sb("rstd", [128, NT], F32)
        sg = sb("sg", [128, 2, NT], BF16)
        abuf = sb("abuf", [128, G, NT], BF16)
        stg = sb("stg", [128, NSTG, 1024], F32)
        ring = sb("ring", [128, RING, 1024], BF16)
        ccs = sb("ccs", [128, NT], F32)
        cbs = sb("cbs", [128, NT], F32)
        y1 = sb("y1", [128, NT], F32)
        ubP = sb("ubP", [128, 1, NT + 2], F32)
        ubS = sb("ubS", [128, 4, 6], F32)
        uhist = sb("uhist", [128, 8, 2], F32)
        sth = sb("sth", [128, 64], F32)
        cvs = sb("cvs", [128, 64], F32)
        kst = sb("kst", [128, 2, NT], F32)
        vst = sb("vst", [128, 2, NT], F32)
        zb = sb("zb", [128, KC, NT], BF16)
        qT = sb("qT", [128, 2, NT], F32)
        pTt = sb("pTt", [128, 3, NT], F32)
        kvk = sb("kvk", [128, 4, 128], F32)
        kvv = sb("kvv", [128, 4, 128], F32)
        rden = sb("rden", [128, NT], F32)
        lft = sb("lft", [128, 2, 8], F32)
        lfe = sb("lfe", [128, 8], F32)
        lfS = sb("lfS", [128, 8], F32)
        Ftm = sb("Ftm", [128, KC + 1, 8], F32)
        negF = sb("negF", [128, KC, 8], F32)
        Fx = sb("Fx", [8, NT], F32)
        Fr = sb("Fr", [8, NT], F32)
        Fhi = sb("Fhi", [8, 3, NT], BF16)
        wfb = sb("wfb", [128, KC, 8], BF16)
        wfs = sb("wfs", [128, KC, 8], F32)
        ptb = sb("ptb", [128, 2, NT], BF16)
        ptbc = sb("ptbc", [128, 2, NPG], I32)
        idxb = sb("idxb", [128, 2, NPG], I32)
        pcolb = sb("pcolb", [64, 2], I32)
        Lb = sb("Lb", [64, 128], F32)
        LT = sb("LT", [128, 64], F32)
        Dm = sb("Dm", [128, 64], F32)
        Csc = sb("Csc", [128, 64], F32)
        ones64 = sb("ones64", [128, 64], F32)
        kvp = sb("kvp", [128, 3, PCH, 256], F32)
        Pm = sb("Pm", [128, 3, PCH * 4], F32)
        tmpS = sb("tmpS", [128, PCH * 4], F32)
        qTs = sb("qTs", [128, NH, NS], F32)
        knT = sb("knT", [128, NH, NS], F32)
        vnw = sb("vnw", [NS, CONV], F32)
        negE = sb("negE", [NS, 8], F32)
        osb = sb("osb", [128, 128], F32)
        idx0f = sb("idx0f", [128, NPG], F32)
        idxh = sb("idxh", [128, NH, NPG], I32)
        pcolh = sb("pcolh", [64, NH], I32)
        psum = es.enter_context(nc.psum_tensor("psum", [128, 8, 512], F32))

        esem = {e: es.enter_context(nc.semaphore("es_" + e)) for e in ("pe", "act", "dve", "pool", "sp")}
        dsem = {}

        def pcol(name, i=0, w=1):
            o, _ = PC[name]
            return prm[:, o + i:o + i + w]

        bank_ctr = [0]

        def nb():
            b = bank_ctr[0] % 4
            bank_ctr[0] += 1
            return b

        acc_ctr = [0]

        def accpair():
            k = acc_ctr[0] % 2
            acc_ctr[0] += 1
            return 4 + 2 * k, 5 + 2 * k

        def dma(fn, reads=(), writes=(), sem=None, eng="sp"):
            if sem not in dsem:
                dsem[sem] = es.enter_context(nc.semaphore("ds_" + sem))
            return S.add(eng, fn, reads=reads, writes=writes, dma=sem)

        p_full = list(range(NU))
        cut = NU_FFN + NU_IN
        plan = p_full * (NBLK + 1)
        if not WITH_SAMPLE:
            plan = p_full * NBLK
        wstate = {"next_use": 0, "next_fetch": 0}

        def fetch_upto(u_hi):
            while wstate["next_fetch"] < min(u_hi, len(plan)):
                u = wstate["next_fetch"]
                wstate["next_fetch"] += 1
                s4, s16, ui = u % NSTG, u % RING, plan[u]
                srcu = wst[ui]
                dma(lambda e, s4=s4, srcu=srcu: e.dma_start(out=stg[:, s4, :], in_=srcu),
                    writes=["stg%d" % s4], sem="wst%d" % s4)
                if u % 3 == 2:
                    S.add("dve", lambda e, s4=s4, s16=s16: e.tensor_copy(out=ring[:, s16, :], in_=stg[:, s4, :]),
                          reads=["stg%d" % s4], writes=["wr%d" % s16])
                else:
                    S.add("act", lambda e, s4=s4, s16=s16: e.activation(out=ring[:, s16, :], in_=stg[:, s4, :],
                                                                        func=AF.Copy),
                          reads=["stg%d" % s4], writes=["wr%d" % s16])

        def take(n):
            u0 = wstate["next_use"]
            wstate["next_use"] += n
            assert n <= RING
            fetch_upto(max(u0 + n, u0 + RING))
            return [(u % RING) for u in range(u0, u0 + n)]

        dma(lambda e: e.dma_start(out=prm[:], in_=par[:]), writes=["prm"], sem="prm")
        S.add("dve", lambda e: e.tensor_copy(out=identb[:], in_=pcol("ident", 0, 128)), reads=["prm"], writes=["identb"])
        S.add("dve", lambda e: e.tensor_copy(out=maskb[:], in_=pcol("mask", 0, 128)), reads=["prm"], writes=["maskb"])
        S.add("dve", lambda e: e.tensor_copy(out=selb[:], in_=prm[0:8, PC["sel"][0]:PC["sel"][0] + 1024]),
              reads=["prm"], writes=["selb"])
        S.add("dve", lambda e: e.memset(onesb[:], 1.0), writes=["onesb"])
        S.add("dve", lambda e: e.memset(onesf[:], 1.0), writes=["onesf"])
        S.add("dve", lambda e: e.memset(ones64[:], 1.0), writes=["ones64"])
        S.add("dve", lambda e: e.memset(epst[:], EPS), writes=["epst"])
        S.add("dve", lambda e: e.memset(Ftm[:, 0, :], 0.0), writes=["Ftm0"])
        S.add("dve", lambda e: e.memset(uhist[:], 0.0), writes=["uhist"])
        dma(lambda e: e.dma_start(out=wfs[:].rearrange("p a b -> p (a b)"), in_=wfd[:]), writes=["wfs"], sem="wf")
        S.add("dve", lambda e: e.tensor_copy(out=wfb[:], in_=wfs[:]), reads=["wfs"], writes=["wfb"])
        dma(lambda e: e.dma_start(out=sth[:], in_=stc[:]), writes=["sth"], sem="sth")

        def mm(out_ap, lhsT, rhs, start, stop, reads, bank):
            S.add("pe", lambda e: e.matmul(out_ap, lhsT, rhs, start=start, stop=stop),
                  reads=reads, writes=["ps%d" % bank])

        def sumsq_rstd(N, nchunks, src, srcreg, dim):
            b = nb()
            for c in range(nchunks):
                sl = c % 2
                S.add("act", lambda e, c=c, sl=sl: e.activation(out=sqr[:, sl, :N], in_=src[:, c, :N], func=AF.Square),
                      reads=["%s%d" % (srcreg, c)], writes=["sqr%d" % sl])
                mm(psum[:, b, :N], onesb[:], sqr[:, sl, :N], c == 0, c == nchunks - 1, ["sqr%d" % sl, "onesb"], b)
            S.add("act", lambda e: e.activation(out=rs[:, :N], in_=psum[:, b, :N], func=AF.Sqrt, bias=epst[:],
                                                scale=1.0 / dim), reads=["ps%d" % b, "epst"], writes=["rs"])
            S.add("dve", lambda e: e.reciprocal(out=rstd[:, :N], in_=rs[:, :N]), reads=["rs"], writes=["rstd"])

        def norm(N, gname, nchunks, src, srcreg, dst, dstreg, dim, c0=0):
            sv = src[:, c0:c0 + nchunks, :]
            sumsq_rstd(N, nchunks, sv, srcreg + "_" if False else srcreg, dim) if c0 == 0 else None
            if c0 != 0:
                b = nb()
                for c in range(nchunks):
                    sl = c % 2
                    S.add("act", lambda e, c=c, sl=sl: e.activation(out=sqr[:, sl, :N], in_=src[:, c0 + c, :N],
                                                                     func=AF.Square),
                          reads=["%s%d" % (srcreg, c0 + c)], writes=["sqr%d" % sl])
                    mm(psum[:, b, :N], onesb[:], sqr[:, sl, :N], c == 0, c == nchunks - 1, ["sqr%d" % sl, "onesb"], b)
                S.add("act", lambda e: e.activation(out=rs[:, :N], in_=psum[:, b, :N], func=AF.Sqrt, bias=epst[:],
                                                    scale=1.0 / dim), reads=["ps%d" % b, "epst"], writes=["rs"])
                S.add("dve", lambda e: e.reciprocal(out=rstd[:, :N], in_=rs[:, :N]), reads=["rs"], writes=["rstd"])
            for c in range(nchunks):
                S.add("dve", lambda e, c=c: e.scalar_tensor_tensor(out=dst[:, c0 + c, :N], in0=src[:, c0 + c, :N],
                                                                    scalar=pcol(gname, c), in1=rstd[:, :N],
                                                                    op0=ALU.mult, op1=ALU.mult),
                      reads=["%s%d" % (srcreg, c0 + c), "rstd", "prm"], writes=["%s%d" % (dstreg, c0 + c)])

        def fm_proj(N, units, src_tile, srcreg, nk=16):
            b = nb()
            for kc in range(nk):
                slot = units[kc // 8]
                mm(psum[:, b, :N], ring[:, slot, (kc % 8) * 128:(kc % 8 + 1) * 128], src_tile[:, kc, :N],
                   kc == 0, kc == nk - 1, ["wr%d" % slot, "%s%d" % (srcreg, kc)], b)
            return b

        def ffn(N, gname):
            norm(N, gname, KC, h, "h", hn, "hn", D)
            for g in range(NG):
                for j in range(G):
                    us = take(4)
                    bg = fm_proj(N, us[0:2], hn, "hn")
                    bu = fm_proj(N, us[2:4], hn, "hn")
                    par2 = j % 2
                    S.add("act", lambda e, bg=bg, par2=par2: e.activation(out=sg[:, par2, :N], in_=psum[:, bg, :N],
                                                                            func=AF.Silu),
                          reads=["ps%d" % bg], writes=["sg%d" % par2])
                    S.add("dve", lambda e, bu=bu, par2=par2, j=j: e.tensor_tensor(
                        out=abuf[:, j, :N], in0=sg[:, par2, :N], in1=psum[:, bu, :N], op=ALU.mult),
                        reads=["sg%d" % par2, "ps%d" % bu], writes=["a%d" % j])
                for half in range(2):
                    us = take(G)
                    for dcl in range(8):
                        dc = half * 8 + dcl
                        b = nb()
                        for j in range(G):
                            mm(psum[:, b, :N], ring[:, us[j], dcl * 128:(dcl + 1) * 128], abuf[:, j, :N],
                               j == 0, j == G - 1, ["wr%d" % us[j], "a%d" % j], b)
                        S.add("dve", lambda e, b=b, dc=dc: e.scalar_tensor_tensor(
                            out=h[:, dc, :N], in0=psum[:, b, :N], scalar=0.5, in1=h[:, dc, :N],
                            op0=ALU.mult, op1=ALU.add),
                            reads=["ps%d" % b, "h%d" % dc], writes=["h%d" % dc])

        def load_block(N, src_ap, t0):
            dma(lambda e: e.dma_start(out=h[:, :, :N],
                                      in_=src_ap.rearrange("(kc p) t -> p kc t", p=128)[:, :, t0:t0 + N]),
                writes=["h%d" % c for c in range(KC)], sem="ldh")

        def mixer_in(N, blk, sample):
            norm(N, "gm", KC, h, "h", hn, "hn", D)
            TS = min(128, N)
            nsub = N // TS
            t0 = 0 if sample else blk * NT
            for s in range(nsub):
                b = nb()
                lsl = s % 2
                for kc in range(KC):
                    mm(psum[:TS, b, 0:8], hn[:, kc, s * TS:(s + 1) * TS], wfb[:, kc, :], kc == 0, kc == KC - 1,
                       ["hn%d" % kc, "wfb"], b)
                S.add("dve", lambda e, b=b: e.tensor_tensor(out=lfe[:TS], in0=psum[:TS, b, 0:8], in1=pcol("bf", 0, 8)[:TS],
                                                            op=ALU.add), reads=["ps%d" % b, "prm"], writes=["lfe"])
                S.add("act", lambda e: e.activation(out=lfe[:TS], in_=lfe[:TS], func=AF.Exp, scale=-1.0),
                      reads=["lfe"], writes=["lfe"])
                S.add("act", lambda e: e.activation(out=lfe[:TS], in_=lfe[:TS], func=AF.Ln, bias=1.0),
                      reads=["lfe"], writes=["lfe"])
                S.add("dve", lambda e, lsl=lsl: e.tensor_scalar(out=lft[:TS, lsl, :], in0=lfe[:TS], scalar1=-1.0,
                                                                 scalar2=None, op0=ALU.mult),
                      reads=["lfe"], writes=["lft%d" % lsl])
                if sample:
                    dma(lambda e, lsl=lsl: e.dma_start(out=lfso[:, :], in_=lft[:TS, lsl, :]),
                        reads=["lft%d" % lsl], sem="olf%d" % lsl)
                    S.add("dve", lambda e, lsl=lsl: e.tensor_copy(out=lfS[:TS], in_=lft[:TS, lsl, :]),
                          reads=["lft%d" % lsl], writes=["lfS"])
                else:
                    sc = blk * 4 + s
                    dma(lambda e, sc=sc, lsl=lsl: e.dma_start(out=lfo[sc * 128:(sc + 1) * 128, :], in_=lft[:, lsl, :]),
                        reads=["lft%d" % lsl], sem="olf%d" % lsl)
                    b2 = nb()
                    mm(psum[:, b2, 0:8], pcol("tri", 0, 128), lft[:, lsl, :], True, False, ["lft%d" % lsl, "prm"], b2)
                    mm(psum[:, b2, 0:8], pcol("last", 0, 128), Ftm[:, sc, :], False, True, ["Ftm%d" % sc, "prm"], b2)
                    S.add("dve", lambda e, b2=b2, sc=sc: e.tensor_copy(out=Ftm[:, sc + 1, :], in_=psum[:, b2, 0:8]),
                          reads=["ps%d" % b2], writes=["Ftm%d" % (sc + 1)])
                    S.add("act", lambda e, b2=b2, sc=sc: e.activation(out=negF[:, sc, :], in_=psum[:, b2, 0:8],
                                                                       func=AF.Copy, scale=-1.0),
                          reads=["ps%d" % b2], writes=["negF%d" % sc])
            if not sample:
                b = nb()
                for s in range(4):
                    sc = blk * 4 + s
                    S.add("pe", lambda e, b=b, s=s, sc=sc: e.transpose(psum[0:8, b, s * 128:(s + 1) * 128],
                                                                        Ftm[:, sc + 1, :], pcol("ident", 0, 128)),
                          reads=["Ftm%d" % (sc + 1), "prm"], writes=["ps%d" % b])
                S.add("dve", lambda e, b=b: e.tensor_scalar(out=Fx[:], in0=psum[0:8, b, :], scalar1=1.0 / SCALE,
                                                            scalar2=None, op0=ALU.mult), reads=["ps%d" % b], writes=["Fx"])
                S.add("dve", lambda e: e.tensor_copy(out=Fhi[:, 0, :], in_=Fx[:]), reads=["Fx"], writes=["Fhi0"])
                S.add("dve", lambda e: e.tensor_tensor(out=Fr[:], in0=Fx[:], in1=Fhi[:, 0, :], op=ALU.subtract),
                      reads=["Fx", "Fhi0"], writes=["Fr"])
                S.add("dve", lambda e: e.tensor_copy(out=Fhi[:, 1, :], in_=Fr[:]), reads=["Fr"], writes=["Fhi1"])
                S.add("dve", lambda e: e.tensor_tensor(out=Fx[:], in0=Fr[:], in1=Fhi[:, 1, :], op=ALU.subtract),
                      reads=["Fr", "Fhi1"], writes=["Fx"])
                S.add("dve", lambda e: e.tensor_copy(out=Fhi[:, 2, :], in_=Fx[:]), reads=["Fx"], writes=["Fhi2"])
            if sample:
                ub, L = ubS, 4
            else:
                ub, L = ubP, NT
            for i in range(8):
                us = take(6)
                bcc = fm_proj(N, us[0:2], hn, "hn")
                S.add("act", lambda e, b=bcc: e.activation(out=ccs[:, :N], in_=psum[:, b, :N], func=AF.Copy),
                      reads=["ps%d" % bcc], writes=["ccs"])
                bch = fm_proj(N, us[2:4], hn, "hn")
                if sample:
                    S.add("dve", lambda e, i=i: e.tensor_copy(
                        out=ub[:, :, 0:2], in_=sth[:, i * 8:(i + 1) * 8].rearrange("p (b j) -> p b j", j=2)),
                        reads=["sth", "ub"], writes=["ub"])
                else:
                    S.add("dve", lambda e, i=i: e.tensor_copy(out=ub[:, 0, 0:2], in_=uhist[:, i, :]),
                          reads=["uhist", "ub"], writes=["ub"])
                S.add("dve", lambda e, b=bch: e.tensor_tensor(
                    out=ub[:, :, 2:L + 2], in0=ccs[:, :N].rearrange("p (b l) -> p b l", l=L),
                    in1=psum[:, b, :N].rearrange("p (b l) -> p b l", l=L), op=ALU.mult),
                    reads=["ccs", "ps%d" % bch, "ub"], writes=["ub"])
                bcb = fm_proj(N, us[4:6], hn, "hn")
                S.add("act", lambda e, b=bcb: e.activation(out=cbs[:, :N], in_=psum[:, b, :N], func=AF.Copy),
                      reads=["ps%d" % bcb], writes=["cbs"])
                y13 = y1[:, :N].rearrange("p (b l) -> p b l", l=L)
                S.add("dve", lambda e, i=i, y13=y13: e.tensor_scalar(out=y13, in0=ub[:, :, 0:L], scalar1=pcol("cw", i),
                                                                     scalar2=None, op0=ALU.mult),
                      reads=["ub", "prm"], writes=["y1"])
                S.add("dve", lambda e, i=i, y13=y13: e.scalar_tensor_tensor(
                    out=y13, in0=ub[:, :, 1:L + 1], scalar=pcol("cw", 8 + i), in1=y13, op0=ALU.mult, op1=ALU.add),
                    reads=["ub", "y1", "prm"], writes=["y1"])
                S.add("dve", lambda e, i=i, y13=y13: e.scalar_tensor_tensor(
                    out=y13, in0=ub[:, :, 2:L + 2], scalar=pcol("cw", 16 + i), in1=y13, op0=ALU.mult, op1=ALU.add),
                    reads=["ub", "y1", "prm"], writes=["y1"])
                S.add("dve", lambda e, i=i: e.tensor_tensor(out=zb[:, i, :N], in0=y1[:, :N], in1=cbs[:, :N], op=ALU.mult),
                      reads=["y1", "cbs"], writes=["zb%d" % i])
                if sample:
                    S.add("dve", lambda e, i=i: e.tensor_copy(
                        out=cvs[:, i * 8:(i + 1) * 8].rearrange("p (b j) -> p b j", j=2), in_=ub[:, :, L:L + 2]),
                        reads=["ub"], writes=["cvs"])
                else:
                    S.add("dve", lambda e, i=i: e.tensor_copy(out=uhist[:, i, :], in_=ub[:, 0, L:L + 2]),
                          reads=["ub"], writes=["uhist"])
            if sample:
                dma(lambda e: e.dma_start(out=cvso[:], in_=cvs[:]), reads=["cvs"], sem="ocvs")
            elif blk == NBLK - 1:
                dma(lambda e: e.dma_start(out=cvo[:], in_=uhist[:].rearrange("p a b -> p (a b)")),
                    reads=["uhist"], sem="ocv")
            norm(N, "gc", 8, zb, "zb", zb, "zb", CONV)
            for hh in range(NH):
                us = take(2)
                bk = fm_proj(N, us, hn, "hn")
                ksl = hh % 2
                S.add("act", lambda e, b=bk, ksl=ksl: e.activation(out=kst[:, ksl, :N], in_=psum[:, b, :N], func=AF.Copy),
                      reads=["ps%d" % bk], writes=["kst%d" % ksl])
                dst = ksT if sample else kT
                dma(lambda e, hh=hh, ksl=ksl, dst=dst: e.dma_start(
                    out=dst[hh * 128:(hh + 1) * 128, t0:t0 + N], in_=kst[:, ksl, :N]),
                    reads=["kst%d" % ksl], writes=[] if sample else ["kTd_%d_%d" % (hh, blk)], sem="ok%d" % ksl)
                if sample:
                    S.add("dve", lambda e, hh=hh, ksl=ksl: e.tensor_copy(out=knT[:, hh, :], in_=kst[:, ksl, :N]),
                          reads=["kst%d" % ksl], writes=["knT"])
            for cb in range(2):
                banks = [nb() for _ in range(nsub)]
                for u in range(8):
                    us = take(1)
                    for s in range(nsub):
                        for kl in range(2):
                            kc = 2 * u + kl
                            mm(psum[:TS, banks[s], :], hn[:, kc, s * TS:(s + 1) * TS],
                               ring[:, us[0], kl * 512:(kl + 1) * 512], kc == 0, kc == KC - 1,
                               ["hn%d" % kc, "wr%d" % us[0]], banks[s])
                for s in range(nsub):
                    vsl = s % 2
                    S.add("act", lambda e, b=banks[s], vsl=vsl: e.activation(
                        out=vst[:TS, vsl, :], in_=psum[:TS, b, :], func=AF.Copy),
                        reads=["ps%d" % banks[s]], writes=["vst%d" % vsl])
                    dst = vso if sample else vo
                    r0 = t0 + s * TS
                    dma(lambda e, vsl=vsl, cb=cb, dst=dst, r0=r0: e.dma_start(
                        out=dst[r0:r0 + TS, cb * 512:(cb + 1) * 512], in_=vst[:TS, vsl, :]),
                        reads=["vst%d" % vsl], writes=[] if sample else ["vod_%d_%d" % (blk * 4 + s, cb)],
                        sem="ov%d" % vsl)
                    if sample:
                        S.add("dve", lambda e, vsl=vsl, cb=cb: e.tensor_copy(out=vnw[:, cb * 512:(cb + 1) * 512],
                                                                             in_=vst[:TS, vsl, :]),
                              reads=["vst%d" % vsl], writes=["vnw"])

        kvctr = [0]
        pctr = [0]

        def attention_prompt(blk):
            N = NT
            nsc = 4 * blk + 4
            for hh in range(NH):
                us = take(2)
                bq = fm_proj(N, us, hn, "hn")
                qsl = hh % 2
                S.add("act", lambda e, b=bq, qsl=qsl: e.activation(out=qT[:, qsl, :], in_=psum[:, b, :], func=AF.Copy),
                      reads=["ps%d" % bq], writes=["qT%d" % qsl])
                bo, bd = accpair()
                for sc in range(nsc):
                    diag = sc >= 4 * blk
                    cs = (sc - 4 * blk) * 128 if diag else 0
                    slot = kvctr[0] % 4
                    kvctr[0] += 1
                    dma(lambda e, slot=slot, hh=hh, sc=sc: e.dma_start(
                        out=kvk[:, slot, :], in_=kT[hh * 128:(hh + 1) * 128, sc * 128:(sc + 1) * 128]),
                        reads=["kTd_%d_%d" % (hh, sc // 4)], writes=["kvk%d" % slot], sem="kvk%d" % slot)
                    dma(lambda e, slot=slot, hh=hh, sc=sc: e.dma_start(
                        out=kvv[:, slot, :], in_=vo[sc * 128:(sc + 1) * 128, hh * 128:(hh + 1) * 128]),
                        reads=["vod_%d_%d" % (sc, hh // 4)], writes=["kvv%d" % slot], sem="kvv%d" % slot)
                    bs = nb()
                    mm(psum[:, bs, cs:N], kvk[:, slot, :], qT[:, qsl, cs:N], True, False,
                       ["kvk%d" % slot, "qT%d" % qsl], bs)
                    for part in range(3):
                        mm(psum[:, bs, cs:N], selb[:, hh * 128:(hh + 1) * 128], Fhi[:, part, cs:N], False,
                           (part == 2 and not diag), ["selb", "Fhi%d" % part], bs)
                    if diag:
                        mm(psum[:, bs, cs:cs + 128], identb[:], maskb[:], False, True, ["identb", "maskb"], bs)
                    psl = pctr[0] % 3
                    pctr[0] += 1
                    S.add("act", lambda e, bs=bs, psl=psl, cs=cs, sc=sc, hh=hh: e.activation(
                        out=pTt[:, psl, cs:N], in_=psum[:, bs, cs:N], func=AF.Exp, bias=negF[:, sc, hh:hh + 1],
                        scale=SCALE), reads=["ps%d" % bs, "negF%d" % sc], writes=["pT%d" % psl])
                    mm(psum[:, bo, cs:N], kvv[:, slot, :], pTt[:, psl, cs:N], sc == 0, sc == nsc - 1,
                       ["kvv%d" % slot, "pT%d" % psl], bo)
                    mm(psum[:, bd, cs:N], onesf[:], pTt[:, psl, cs:N], sc == 0, sc == nsc - 1,
                       ["onesf", "pT%d" % psl], bd)
                S.add("dve", lambda e, bd=bd: e.reciprocal(out=rden[:], in_=psum[:, bd, :]),
                      reads=["ps%d" % bd], writes=["rden"])
                S.add("dve", lambda e, bo=bo, hh=hh: e.tensor_tensor(out=zb[:, 8 + hh, :], in0=psum[:, bo, :],
                                                                     in1=rden[:], op=ALU.mult),
                      reads=["ps%d" % bo, "rden"], writes=["zb%d" % (8 + hh)])

        def mixer_out(N):
            norm(N, "ga", 8, zb, "zb", zb, "zb", CONV, c0=8)
            for dc in range(KC):
                us = take(2)
                b = fm_proj(N, us, zb, "zb")
                S.add("dve", lambda e, b=b, dc=dc: e.tensor_tensor(out=h[:, dc, :N], in0=psum[:, b, :N],
                                                                   in1=h[:, dc, :N], op=ALU.add),
                      reads=["ps%d" % b, "h%d" % dc], writes=["h%d" % dc])

        def ple(N, p_ap, t0):
            norm(N, "gp", KC, h, "h", hn, "hn", D)
            dma(lambda e: e.dma_start(out=kst[:, :, :N], in_=p_ap.rearrange("(kc p) t -> p kc t", p=128)[:, :, t0:t0 + N]),
                writes=["kst0", "kst1"], sem="ldp")
            S.add("dve", lambda e: e.tensor_copy(out=ptb[:, :, :N], in_=kst[:, :, :N]), reads=["kst0", "kst1"],
                  writes=["ptb"])
            for q in range(4):
                pu = take(1)[0]
                pbanks = []
                for dcl in range(4):
                    b = 4 + dcl
                    pbanks.append(b)
                    for kc in range(2):
                        mm(psum[:, b, :N], ring[:, pu, (dcl * 2 + kc) * 128:(dcl * 2 + kc + 1) * 128], ptb[:, kc, :N],
                           kc == 0, kc == 1, ["wr%d" % pu, "ptb"], b)
                for dcl in range(4):
                    dc = 4 * q + dcl
                    us = take(2)
                    bgt = fm_proj(N, us, hn, "hn")
                    S.add("act", lambda e, b=bgt: e.activation(out=ccs[:, :N], in_=psum[:, b, :N], func=AF.Sigmoid),
                          reads=["ps%d" % bgt], writes=["ccs"])
                    pb = pbanks[dcl]
                    S.add("dve", lambda e, pb=pb: e.tensor_tensor(out=y1[:, :N], in0=ccs[:, :N], in1=psum[:, pb, :N],
                                                                  op=ALU.mult), reads=["ccs", "ps%d" % pb], writes=["y1"])
                    S.add("dve", lambda e, dc=dc: e.tensor_tensor(out=h[:, dc, :N], in0=h[:, dc, :N], in1=y1[:, :N],
                                                                  op=ALU.add), reads=["y1", "h%d" % dc], writes=["h%d" % dc])

        def final_out(N, dst, t0):
            sumsq_rstd(N, KC, h, "h", D)
            for c in range(KC):
                sl = c % 2
                S.add("dve", lambda e, c=c, sl=sl: e.scalar_tensor_tensor(
                    out=vst[:, sl, :N], in0=h[:, c, :N], scalar=pcol("gf", c), in1=rstd[:, :N],
                    op0=ALU.mult, op1=ALU.mult), reads=["h%d" % c, "rstd", "prm"], writes=["vst%d" % sl])
                dma(lambda e, c=c, sl=sl: e.dma_start(out=dst[c * 128:(c + 1) * 128, t0:t0 + N], in_=vst[:, sl, :N]),
                    reads=["vst%d" % sl], sem="ov%d" % sl)

        def attention_sample():
            N = NS
            for hh in range(NH):
                us = take(2)
                b = fm_proj(N, us, hn, "hn")
                S.add("act", lambda e, b=b, hh=hh: e.activation(out=qTs[:, hh, :], in_=psum[:, b, :N], func=AF.Copy),
                      reads=["ps%d" % b], writes=["qTs"])
            b = nb()
            btri16 = prm[0:NS, PC["btri"][0]:PC["btri"][0] + NS]
            mm(psum[:NS, b, 0:8], btri16, lfS[:NS, :], True, True, ["lfS", "prm"], b)
            S.add("act", lambda e, b=b: e.activation(out=negE[:], in_=psum[:NS, b, 0:8], func=AF.Copy, scale=-1.0),
                  reads=["ps%d" % b], writes=["negE"])
            bo, bd = accpair()
            cctr = 0
            for bi in range(4):
                ib = bi % 2
                dma(lambda e, bi=bi, ib=ib: e.dma_start(out=ptbc[:, ib, :], in_=ptab[bi:bi + 1, :].partition_broadcast(128)),
                    writes=["ptbc%d" % ib], sem="ptb%d" % ib)
                dma(lambda e, bi=bi, ib=ib: e.dma_start(out=pcolb[:, ib:ib + 1],
                                                        in_=ptab[bi:bi + 1, :].rearrange("a (p f) -> p (a f)", f=1)),
                    writes=["pcolb%d" % ib], sem="ptc%d" % ib)
                S.add("dve", lambda e, ib=ib: e.tensor_scalar(out=idx0f[:], in0=ptbc[:, ib, :], scalar1=128.0,
                                                               scalar2=pcol("iota", 0, 1), op0=ALU.mult, op1=ALU.add),
                      reads=["ptbc%d" % ib, "prm"], writes=["idx0f"])
                for hh in range(NH):
                    S.add("dve", lambda e, hh=hh: e.tensor_scalar(out=idxh[:, hh, :], in0=idx0f[:],
                                                                   scalar1=float(hh * NPOOL * 128), scalar2=None,
                                                                   op0=ALU.add),
                          reads=["idx0f"], writes=["idxh%d" % hh])
                    S.add("dve", lambda e, hh=hh, ib=ib: e.tensor_scalar(out=pcolh[:, hh:hh + 1], in0=pcolb[:, ib:ib + 1],
                                                                          scalar1=float(hh * NPOOL), scalar2=None,
                                                                          op0=ALU.add),
                          reads=["pcolb%d" % ib], writes=["pcolh%d" % hh])
                for hh in range(NH):
                    col0 = (bi * NH + hh) * 4
                    dcol0 = (bi * NH + hh) * 4 * PCH
                    dma(lambda e, hh=hh: e.indirect_dma_start(
                        out=Lb[:, :], out_offset=None, in_=clf[:, :],
                        in_offset=bass.IndirectOffsetOnAxis(ap=pcolh[:, hh:hh + 1], axis=0)),
                        reads=["pcolh%d" % hh], writes=["Lb"], sem="lb", eng="pool")
                    b = nb()
                    S.add("pe", lambda e, b=b: e.transpose(psum[:, b, 0:64], Lb[:, :],
                                                            prm[0:64, PC["ident"][0]:PC["ident"][0] + 64]),
                          reads=["Lb", "prm"], writes=["ps%d" % b])
                    S.add("dve", lambda e, b=b: e.tensor_copy(out=LT[:], in_=psum[:, b, 0:64]),
                          reads=["ps%d" % b], writes=["LT"])
                    ba, bt = nb(), nb()
                    mm(psum[:, ba, 0:64], pcol("ustr", 0, 128), LT[:], True, True, ["LT", "prm"], ba)
                    mm(psum[:, bt, 0:64], onesf[:], LT[:], True, True, ["LT", "onesf"], bt)
                    S.add("dve", lambda e, bt=bt: e.tensor_copy(out=Dm[:], in_=psum[:, bt, 0:64]),
                          reads=["ps%d" % bt], writes=["Dm"])
                    S.add("dve", lambda e: e.tensor_tensor_scan(out=Csc[:], data0=ones64[:], data1=Dm[:], initial=0.0,
                                                                op0=ALU.mult, op1=ALU.add),
                          reads=["Dm", "ones64"], writes=["Csc"])
                    S.add("dve", lambda e, ba=ba: e.scalar_tensor_tensor(out=Dm[:], in0=Csc[:], scalar=-1.0,
                                                                         in1=psum[:, ba, 0:64], op0=ALU.mult, op1=ALU.add),
                          reads=["Csc", "ps%d" % ba, "Dm"], writes=["Dm"])
                    S.add("dve", lambda e: e.tensor_scalar(out=Dm[:], in0=Dm[:], scalar1=Csc[:, 63:64], scalar2=None,
                                                           op0=ALU.add), reads=["Dm", "Csc"], writes=["Dm"])
                    for ch in range(NPG // PCH):
                        sl = cctr % 3
                        cctr += 1
                        for pp in range(PCH):
                            j = ch * PCH + pp
                            dma(lambda e, hh=hh, j=j, sl=sl, pp=pp: e.indirect_dma_start(
                                out=kvp[:, sl, pp, :], out_offset=None, in_=ckv[:, :],
                                in_offset=bass.IndirectOffsetOnAxis(ap=idxh[:, hh, j:j + 1], axis=0)),
                                reads=["idxh%d" % hh], writes=["kvp%d_%d" % (sl, pp)], sem="pg%d_%d" % (sl, pp), eng="pool")
                        bs = nb()
                        for pp in range(PCH):
                            mm(psum[:, bs, pp * 4:(pp + 1) * 4], kvp[:, sl, pp, 0:128],
                               qTs[:, hh, bi * 4:(bi + 1) * 4], True, True, ["kvp%d_%d" % (sl, pp), "qTs"], bs)
                        S.add("dve", lambda e, bs=bs, ch=ch: e.scalar_tensor_tensor(
                            out=tmpS[:].rearrange("p (a t) -> p a t", t=4),
                            in0=psum[:, bs, 0:PCH * 4].rearrange("p (a t) -> p a t", t=4), scalar=SCALE,
                            in1=Dm[:, ch * PCH:(ch + 1) * PCH].unsqueeze(2).to_broadcast([128, PCH, 4]),
                            op0=ALU.mult, op1=ALU.add), reads=["ps%d" % bs, "Dm"], writes=["tmpS"])
                        S.add("act", lambda e, sl=sl: e.activation(out=Pm[:, sl, :], in_=tmpS[:], func=AF.Exp),
                              reads=["tmpS"], writes=["Pm%d" % sl])
                        mm(psum[:, bd, dcol0:dcol0 + 4 * PCH], onesf[:], Pm[:, sl, :], ch == 0, False,
                           ["onesf", "Pm%d" % sl], bd)
                        for pp in range(PCH):
                            first = (ch == 0 and pp == 0)
                            mm(psum[:, bo, col0:col0 + 4], kvp[:, sl, pp, 128:256],
                               Pm[:, sl, pp * 4:(pp + 1) * 4], first, False, ["kvp%d_%d" % (sl, pp), "Pm%d" % sl], bo)
                    bs = nb()
                    mm(psum[:NS, bs, 0:4], knT[:, hh, :], qTs[:, hh, bi * 4:(bi + 1) * 4], True, True, ["knT", "qTs"], bs)
                    S.add("dve", lambda e, bs=bs, bi=bi: e.scalar_tensor_tensor(
                        out=tmpS[:NS, 0:4], in0=psum[:NS, bs, 0:4], scalar=SCALE, in1=pcol("maskS", bi * 4, 4)[:NS],
                        op0=ALU.mult, op1=ALU.add), reads=["ps%d" % bs, "prm"], writes=["tmpS"])
                    sl = cctr % 3
                    cctr += 1
                    S.add("act", lambda e, sl=sl, hh=hh: e.activation(out=Pm[:NS, sl, 0:4], in_=tmpS[:NS, 0:4], func=AF.Exp,
                                                                       bias=negE[:, hh:hh + 1]),
                          reads=["tmpS", "negE"], writes=["Pm%d" % sl])
                    mm(psum[:, bo, col0:col0 + 4], vnw[:, hh * 128:(hh + 1) * 128], Pm[:NS, sl, 0:4], False, True,
                       ["vnw", "Pm%d" % sl], bo)
                    mm(psum[:, bd, dcol0:dcol0 + 4], onesf[:NS, :], Pm[:NS, sl, 0:4], False, True, ["onesf", "Pm%d" % sl], bd)
            S.add("dve", lambda e: e.tensor_reduce(
                out=rs[:, :128].rearrange("p (g t) -> p g t", t=4),
                in_=psum[:, bd, :].rearrange("p (g a t) -> p g t a", a=PCH, t=4),
                axis=mybir.AxisListType.X, op=ALU.add), reads=["ps%d" % bd], writes=["rs"])
            S.add("dve", lambda e: e.reciprocal(out=rden[:, :128], in_=rs[:, :128]), reads=["rs"], writes=["rden"])
            S.add("dve", lambda e: e.tensor_tensor(out=osb[:], in0=psum[:, bo, :128], in1=rden[:, :128], op=ALU.mult),
                  reads=["ps%d" % bo, "rden"], writes=["osb"])
            for hh in range(NH):
                S.add("dve", lambda e, hh=hh: e.tensor_copy(
                    out=zb[:, 8 + hh, 0:NS].rearrange("p (i t) -> p i t", t=4),
                    in_=osb[:].rearrange("p (i h t) -> p i h t", h=NH, t=4)[:, :, hh, :]),
                    reads=["osb"], writes=["zb%d" % (8 + hh)])

        for blk in range(NBLK):
            load_block(NT, xT, blk * NT)
            ffn(NT, "g1")
            mixer_in(NT, blk, False)
            attention_prompt(blk)
            mixer_out(NT)
            ffn(NT, "g2")
            ple(NT, pT, blk * NT)
            final_out(NT, yT, blk * NT)
        if WITH_SAMPLE:
            load_block(NS, xsT, 0)
            ffn(NS, "g1")
            mixer_in(NS, None, True)
            attention_sample()
            mixer_out(NS)
            ffn(NS, "g2")
            ple(NS, psT, 0)
            final_out(NS, ysT, 0)
            assert wstate["next_use"] == len(plan), (wstate["next_use"], len(plan))

        finals = [n for n in dsem if n[0] == "o"]
        S.emit(nc, None, esem, dsem, finals)
    return nc


_CACHE = {}


def kernel(**inp):
    inp = {k: np.asarray(v) for k, v in inp.items()}
    if "nc" not in _CACHE:
        _CACHE["nc"] = build_program()
    nc = _CACHE["nc"]
    wf = np.ascontiguousarray(inp["w_in"][0][:, 6144:6152].reshape(16, 128, 8).transpose(1, 0, 2).reshape(128, 128))
    base = pack_weights(inp, 0)
    if WITH_SAMPLE:
        ck = inp["cache_k"][0].transpose(2, 0, 3, 1)
        cv = inp["cache_v"][0].transpose(2, 0, 1, 3)
        ckv = np.concatenate([ck, cv], axis=3).reshape(NH * NPOOL * 128, 256)
        clf = np.ascontiguousarray(inp["cache_logf"][0].transpose(2, 0, 1)).reshape(NH * NPOOL, 128)
    in_maps = []
    for c in range(8):
        m = {
            "xT": np.ascontiguousarray(inp["x_prompt"][c].T),
            "pT": np.ascontiguousarray(inp["p_prompt"][0, c].T),
            "xsT": np.ascontiguousarray(inp["x_sample"][4 * c:4 * c + 4].reshape(NS, D).T),
            "psT": np.ascontiguousarray(inp["p_sample"][0, 4 * c:4 * c + 4].reshape(NS, PLE).T),
            "stc": np.ascontiguousarray(inp["state_conv"][0, 4 * c:4 * c + 4].reshape(4, 2, 8, 128)
                                        .transpose(3, 2, 0, 1)).reshape(128, 64),
            "wst": base, "par": pack_params(inp, c), "wfd": wf,
        }
        if WITH_SAMPLE:
            m["ckv"] = ckv
            m["clf"] = clf
            m["ptab"] = np.ascontiguousarray(inp["page_table"][4 * c:4 * c + 4].astype(np.int32))
        in_maps.append(m)
    res = run_bass_kernel_spmd(nc, in_maps, core_ids=list(range(8))).results
    y_p = np.stack([res[c]["yT"].T for c in range(8)], 0)
    k_p = np.stack([res[c]["kT"].T.reshape(SEQ, NH, HD) for c in range(8)], 0)[None]
    v_p = np.stack([res[c]["vo"].reshape(SEQ, NH, HD) for c in range(8)], 0)[None]
    lf_p = np.stack([res[c]["lfo"] for c in range(8)], 0)[None]
    cv_p = np.stack([res[c]["cvo"].reshape(128, 8, 2).transpose(2, 1, 0).reshape(2, CONV) for c in range(8)], 0)[None]

    def cat(name, fn):
        return np.concatenate([fn(res[c][name]) for c in range(8)], 0)
    y_s = cat("ysT", lambda a: a.T.reshape(4, 4, D))
    k_s = cat("ksT", lambda a: a.T.reshape(4, 4, NH, HD))[None]
    v_s = cat("vso", lambda a: a.reshape(4, 4, NH, HD))[None]
    lf_s = cat("lfso", lambda a: a.reshape(4, 4, NH))[None]
    cv_s = cat("cvso", lambda a: a.reshape(128, 8, 4, 2).transpose(2, 3, 1, 0).reshape(4, 2, CONV))[None]
    f = np.float32
    return tuple(np.ascontiguousarray(a, f) for a in (y_p, y_s, k_p, v_p, lf_p, cv_p, k_s, v_s, lf_s, cv_s))
```
